# Optimizing a Trainium2 kernel written in Bass

```python
import math
import jax, jax.numpy as jnp
from jax import lax
import numpy as np

D_MODEL = 1024
BATCH = 16
SEQ = 2048
DEPTH = 2

CTX_LEN = 256
GRID_W = 64
D_MIX = 1024
EPS = 1e-6
ROPE_BASE = 10000.0
Q_BLOCK = 128

MLA_HEADS = 6
MLA_NOPE = 64
MLA_ROPE = 32
MLA_V = 64
MLA_Q_RANK = 384
MLA_KV_RANK = 256
MLA_WIDTH = MLA_HEADS * MLA_V
MLA_SCALE = (MLA_NOPE + MLA_ROPE) ** -0.5

GQA_HEADS = 6
GQA_KV_HEADS = 2
GQA_GROUP = GQA_HEADS // GQA_KV_HEADS
GQA_DIM = 64
GQA_WIDTH = GQA_HEADS * GQA_DIM
GQA_SCALE = GQA_DIM ** -0.5

GDN_HEADS = 4
GDN_DK = 64
GDN_DV = 64
GDN_WIDTH = GDN_HEADS * GDN_DV
GDN_CONV = 5
GDN_CHUNK = 64

IN_SPLITS = (
    MLA_Q_RANK,
    MLA_KV_RANK,
    MLA_ROPE,
    GQA_HEADS * GQA_DIM,
    GQA_KV_HEADS * GQA_DIM,
    GQA_KV_HEADS * GQA_DIM,
    3 * GDN_HEADS * GDN_DK,
    2 * GDN_HEADS,
    2 * GDN_HEADS,
    D_MIX,
)
D_IN = sum(IN_SPLITS)

kernel_name = 'hybrid_mla_gqa_gdn_dit_block'


def rms_norm(x, g):
    xf = x.astype(jnp.float32)
    y = xf * lax.rsqrt(jnp.mean(xf * xf, axis=-1, keepdims=True) + EPS)
    return (y * g.astype(jnp.float32)).astype(x.dtype)


def l2norm(x):
    return x * lax.rsqrt(jnp.sum(x * x, axis=-1, keepdims=True) + EPS)


def split_cols(h):
    parts, start = [], 0
    for n in IN_SPLITS:
        parts.append(h[..., start:start + n])
        start += n
    return parts


def rope_1d(x, pos):
    d = x.shape[-1]
    half = d // 2
    inv = ROPE_BASE ** (-jnp.arange(0, d, 2, dtype=jnp.float32) / d)
    ang = pos.astype(jnp.float32)[:, None] * inv[None, :]
    shape = (pos.shape[0],) + (1,) * (x.ndim - 3) + (half,)
    cos, sin = jnp.cos(ang).reshape(shape), jnp.sin(ang).reshape(shape)
    xf = x.astype(jnp.float32)
    x1, x2 = xf[..., :half], xf[..., half:]
    return jnp.concatenate([x1 * cos - x2 * sin, x2 * cos + x1 * sin], -1).astype(x.dtype)


def rope_2d(x, rows, cols):
    h = x.shape[-1] // 2
    return jnp.concatenate([rope_1d(x[..., :h], rows), rope_1d(x[..., h:], cols)], -1)


def attend(q, k, v, scale):
    B, Lq, Hk, G, dq = q.shape
    nb = Lq // Q_BLOCK
    qb = jnp.moveaxis(q.reshape(B, nb, Q_BLOCK, Hk, G, dq), 1, 0)

    def one_block(qi):
        s = jnp.einsum('bqhgd,bkhd->bhgqk', qi, k).astype(jnp.float32) * scale
        p = jax.nn.softmax(s, axis=-1).astype(v.dtype)
        return jnp.einsum('bhgqk,bkhe->bqhge', p, v)

    o = lax.map(one_block, qb)
    return jnp.moveaxis(o, 0, 1).reshape(B, Lq, Hk * G * v.shape[-1])


def mla_q(cq, qn_g, w_uq, pos):
    B, L, _ = cq.shape
    q = (rms_norm(cq, qn_g) @ w_uq).reshape(B, L, MLA_HEADS, MLA_NOPE + MLA_ROPE)
    q_nope, q_rope = q[..., :MLA_NOPE], q[..., MLA_NOPE:]
    if pos is not None:
        q_rope = rope_2d(q_rope, *pos)
    return jnp.concatenate([q_nope, q_rope], -1)[:, :, :, None, :]


def mla_kv(ckv, kr, kvn_g, w_ukv, pos):
    B, L, _ = ckv.shape
    kv = (rms_norm(ckv, kvn_g) @ w_ukv).reshape(B, L, MLA_HEADS, MLA_NOPE + MLA_V)
    k_nope, v = kv[..., :MLA_NOPE], kv[..., MLA_NOPE:]
    kr = kr[:, :, None, :]
    if pos is not None:
        kr = rope_2d(kr, *pos)
    k = jnp.concatenate([k_nope, jnp.broadcast_to(kr, (B, L, MLA_HEADS, MLA_ROPE))], -1)
    return k, v


def gqa_q(q, qn_g, pos):
    B, L, _ = q.shape
    q = rms_norm(q.reshape(B, L, GQA_KV_HEADS, GQA_GROUP, GQA_DIM), qn_g)
    return rope_2d(q, *pos) if pos is not None else q


def gqa_kv(k, v, kn_g, pos):
    B, L, _ = k.shape
    k = rms_norm(k.reshape(B, L, GQA_KV_HEADS, GQA_DIM), kn_g)
    k = rope_2d(k, *pos) if pos is not None else k
    return k, v.reshape(B, L, GQA_KV_HEADS, GQA_DIM)


def short_conv(x, w):
    C = x.shape[-1]
    return lax.conv_general_dilated(
        x, w[:, None, :].astype(x.dtype), window_strides=(1,),
        padding=[(GDN_CONV // 2, GDN_CONV // 2)],
        dimension_numbers=('NWC', 'WIO', 'NWC'), feature_group_count=C)


def gdn_prep(qkv, b, a, conv_w, a_log, dt_bias):
    B, L, _ = qkv.shape
    y = jax.nn.silu(short_conv(qkv, conv_w)).astype(jnp.float32)
    q, k, v = jnp.split(y, 3, axis=-1)
    q = l2norm(q.reshape(B, L, GDN_HEADS, GDN_DK))
    k = l2norm(k.reshape(B, L, GDN_HEADS, GDN_DK))
    v = v.reshape(B, L, GDN_HEADS, GDN_DV)
    beta = jax.nn.sigmoid(b.astype(jnp.float32)).reshape(B, L, 2, GDN_HEADS)
    g = -jnp.exp(a_log.astype(jnp.float32)) * jax.nn.softplus(
        a.astype(jnp.float32).reshape(B, L, 2, GDN_HEADS) + dt_bias.astype(jnp.float32))
    return q, k, v, beta, g


def gated_delta_chunked(q, k, v, g, beta, s0):
    B, L, H, DK = q.shape
    DV = v.shape[-1]
    C = GDN_CHUNK
    N = L // C
    to_chunks = lambda t: jnp.moveaxis(t.reshape(B, N, C, H, *t.shape[3:]), 3, 1)
    q = to_chunks(q) * DK ** -0.5
    k = to_chunks(k)
    v = to_chunks(v)
    beta = to_chunks(beta)
    gc = jnp.cumsum(to_chunks(g), axis=-1)
    idx = jnp.arange(C)
    incl = idx[:, None] >= idx[None, :]
    strict = idx[:, None] > idx[None, :]
    decay = jnp.exp(jnp.where(incl, gc[..., :, None] - gc[..., None, :], -jnp.inf))
    kb = k * beta[..., None]
    kk = jnp.where(strict, jnp.einsum('bhnid,bhnjd->bhnij', kb, k) * decay, 0.0)
    rhs = jnp.concatenate([v * beta[..., None], kb * jnp.exp(gc)[..., None]], -1)
    sol = lax.linalg.triangular_solve(jnp.eye(C, dtype=q.dtype) + kk, rhs,
                                      left_side=True, lower=True)
    u, w = sol[..., :DV], sol[..., DV:]
    a_intra = jnp.einsum('bhnid,bhnjd->bhnij', q, k) * decay
    q_dec = q * jnp.exp(gc)[..., None]
    k_dec = k * jnp.exp(gc[..., -1:] - gc)[..., None]
    g_last = jnp.exp(gc[..., -1])

    def step(S, xs):
        u_i, w_i, a_i, qd_i, kd_i, gl_i = xs
        v_new = u_i - jnp.einsum('bhck,bhkv->bhcv', w_i, S)
        o_i = (jnp.einsum('bhck,bhkv->bhcv', qd_i, S)
               + jnp.einsum('bhcj,bhjv->bhcv', a_i, v_new))
        S = S * gl_i[..., None, None] + jnp.einsum('bhck,bhcv->bhkv', kd_i, v_new)
        return S, o_i

    xs = tuple(jnp.moveaxis(t, 2, 0) for t in (u, w, a_intra, q_dec, k_dec, g_last))
    S, o = lax.scan(step, s0, xs)
    o = jnp.moveaxis(jnp.moveaxis(o, 0, 2), 1, 3).reshape(B, L, H, DV)
    return o, S


def gdn_bidirectional(lat, ctx):
    ql, kl, vl, bl, gl = lat
    qc, kc, vc, bc, gcx = ctx
    s0 = jnp.zeros((qc.shape[0], GDN_HEADS, GDN_DK, GDN_DV), jnp.float32)
    o_l, o_c = None, None
    for d in range(2):
        fl = (lambda t: jnp.flip(t, axis=1)) if d == 1 else (lambda t: t)
        oc_d, s_ctx = gated_delta_chunked(fl(qc), fl(kc), fl(vc), fl(gcx[:, :, d]),
                                          fl(bc[:, :, d]), s0)
        ol_d, _ = gated_delta_chunked(fl(ql), fl(kl), fl(vl), fl(gl[:, :, d]),
                                      fl(bl[:, :, d]), s_ctx)
        o_l = fl(ol_d) if o_l is None else o_l + fl(ol_d)
        o_c = fl(oc_d) if o_c is None else o_c + fl(oc_d)
    return o_l, o_c


def merge_branches(o_mla, o_gqa, o_gdn, gate, on_g, w_out, dtype):
    B, L = o_gdn.shape[:2]
    o_gdn = rms_norm(o_gdn, on_g).reshape(B, L, GDN_WIDTH)
    o = jnp.concatenate([o_mla.astype(dtype), o_gqa.astype(dtype), o_gdn.astype(dtype)], -1)
    return (o * jax.nn.silu(gate)) @ w_out


def hybrid_layer(x, ctx, mod_lat, mod_ctx, norm_g, w_in, mla_qn_g, mla_w_uq, mla_kvn_g,
                 mla_w_ukv, gqa_qn_g, gqa_kn_g, gdn_conv_w, gdn_a_log, gdn_dt_bias,
                 gdn_on_g, w_out, pos, update_ctx):
    sh_l, sc_l, gt_l = jnp.split(mod_lat, 3, axis=-1)
    sh_c, sc_c, gt_c = jnp.split(mod_ctx, 3, axis=-1)
    h_l = rms_norm(x, norm_g) * (1 + sc_l) + sh_l
    h_c = rms_norm(ctx, norm_g) * (1 + sc_c) + sh_c
    (l_cq, l_ckv, l_kr, l_q, l_k, l_v, l_gdn, l_b, l_a, l_gate) = split_cols(h_l @ w_in)
    (c_cq, c_ckv, c_kr, c_q, c_k, c_v, c_gdn, c_b, c_a, c_gate) = split_cols(h_c @ w_in)

    kc_a, vc_a = mla_kv(c_ckv, c_kr, mla_kvn_g, mla_w_ukv, None)
    kl_a, vl_a = mla_kv(l_ckv, l_kr, mla_kvn_g, mla_w_ukv, pos)
    o_mla_l = attend(mla_q(l_cq, mla_qn_g, mla_w_uq, pos),
                     jnp.concatenate([kl_a, kc_a], 1), jnp.concatenate([vl_a, vc_a], 1), MLA_SCALE)

    kc_b, vc_b = gqa_kv(c_k, c_v, gqa_kn_g, None)
    kl_b, vl_b = gqa_kv(l_k, l_v, gqa_kn_g, pos)
    o_gqa_l = attend(gqa_q(l_q, gqa_qn_g, pos),
                     jnp.concatenate([kl_b, kc_b], 1), jnp.concatenate([vl_b, vc_b], 1), GQA_SCALE)

    o_gdn_l, o_gdn_c = gdn_bidirectional(
        gdn_prep(l_gdn, l_b, l_a, gdn_conv_w, gdn_a_log, gdn_dt_bias),
        gdn_prep(c_gdn, c_b, c_a, gdn_conv_w, gdn_a_log, gdn_dt_bias))

    x_new = x + gt_l * merge_branches(o_mla_l, o_gqa_l, o_gdn_l, l_gate, gdn_on_g, w_out, x.dtype)
    if update_ctx:
        o_mla_c = attend(mla_q(c_cq, mla_qn_g, mla_w_uq, None), kc_a, vc_a, MLA_SCALE)
        o_gqa_c = attend(gqa_q(c_q, gqa_qn_g, None), kc_b, vc_b, GQA_SCALE)
        ctx = ctx + gt_c * merge_branches(o_mla_c, o_gqa_c, o_gdn_c, c_gate, gdn_on_g, w_out, ctx.dtype)
    return x_new, ctx


def setup_inputs(seed: int = 0) -> dict:
    key = jax.random.key(seed)
    ks = jax.random.split(key, 24)
    f32 = jnp.float32
    nrm = lambda k, shape, s: s * jax.random.normal(k, shape, f32)
    gain = lambda k, shape: 1.0 + 0.02 * jax.random.normal(k, shape, f32)
    dt = jnp.exp(jax.random.uniform(ks[16], (DEPTH, 2, GDN_HEADS), f32,
                                    minval=math.log(1e-3), maxval=math.log(1e-1)))
    return {
        'x': nrm(ks[0], (BATCH, SEQ, D_MODEL), 1.0),
        'c': nrm(ks[1], (BATCH, D_MODEL), 1.0),
        'ctx': nrm(ks[2], (BATCH, CTX_LEN, D_MODEL), 1.0),
        'c_ctx': nrm(ks[3], (D_MODEL,), 1.0),
        'w_ada': nrm(ks[4], (DEPTH, D_MODEL, 3 * D_MODEL), 0.5 * D_MODEL ** -0.5),
        'b_ada': nrm(ks[5], (DEPTH, 3 * D_MODEL), 0.02),
        'norm_g': gain(ks[6], (DEPTH, D_MODEL)),
        'w_in': nrm(ks[7], (DEPTH, D_MODEL, D_IN), D_MODEL ** -0.5),
        'mla_qn_g': gain(ks[8], (DEPTH, MLA_Q_RANK)),
        'mla_w_uq': nrm(ks[9], (DEPTH, MLA_Q_RANK, MLA_HEADS * (MLA_NOPE + MLA_ROPE)), MLA_Q_RANK ** -0.5),
        'mla_kvn_g': gain(ks[10], (DEPTH, MLA_KV_RANK)),
        'mla_w_ukv': nrm(ks[11], (DEPTH, MLA_KV_RANK, MLA_HEADS * (MLA_NOPE + MLA_V)), MLA_KV_RANK ** -0.5),
        'gqa_qn_g': gain(ks[12], (DEPTH, GQA_DIM)),
        'gqa_kn_g': gain(ks[13], (DEPTH, GQA_DIM)),
        'gdn_conv_w': nrm(ks[14], (DEPTH, GDN_CONV, 3 * GDN_HEADS * GDN_DK), GDN_CONV ** -0.5),
        'gdn_a_log': jnp.log(jax.random.uniform(ks[15], (DEPTH, 2, GDN_HEADS), f32, minval=1.0, maxval=16.0)),
        'gdn_dt_bias': dt + jnp.log(-jnp.expm1(-dt)),
        'gdn_on_g': gain(ks[17], (DEPTH, GDN_DV)),
        'w_out': nrm(ks[18], (DEPTH, D_MIX, D_MODEL), D_MIX ** -0.5),
        'final_g': gain(ks[19], (D_MODEL,)),
    }


def reference(x, c, ctx, c_ctx, w_ada, b_ada, norm_g, w_in, mla_qn_g, mla_w_uq, mla_kvn_g,
              mla_w_ukv, gqa_qn_g, gqa_kn_g, gdn_conv_w, gdn_a_log, gdn_dt_bias, gdn_on_g,
              w_out, final_g):
    seq_len = x.shape[1]
    ROWS = seq_len // GRID_W
    rows = jnp.repeat(jnp.arange(ROWS, dtype=jnp.int32), GRID_W)
    cols = jnp.tile(jnp.arange(GRID_W, dtype=jnp.int32), ROWS)
    pos = (rows, cols)
    s_c = jax.nn.silu(c)
    s_cc = jax.nn.silu(c_ctx)
    for l in range(DEPTH):
        mod_lat = (s_c @ w_ada[l] + b_ada[l])[:, None, :]
        mod_ctx = s_cc @ w_ada[l] + b_ada[l]
        x, ctx = hybrid_layer(x, ctx, mod_lat, mod_ctx, norm_g[l], w_in[l], mla_qn_g[l],
                              mla_w_uq[l], mla_kvn_g[l], mla_w_ukv[l], gqa_qn_g[l], gqa_kn_g[l],
                              gdn_conv_w[l], gdn_a_log[l], gdn_dt_bias[l], gdn_on_g[l], w_out[l],
                              pos, l < DEPTH - 1)
    return rms_norm(x, final_g)
```

```python
import math
import numpy as np
import ml_dtypes
import concourse.bass as bass
import concourse.mybir as mybir
from concourse.bass_utils import run_bass_kernel_spmd

F32 = mybir.dt.float32
BF16 = mybir.dt.bfloat16
AF = mybir.ActivationFunctionType
ALU = mybir.AluOpType
AX = mybir.AxisListType

D = 1024
SEQ = 2048
CTX = 256
NT = SEQ + CTX
NTILE = NT // 128
EPS = 1e-6
MLA_SCALE = 96 ** -0.5
GQA_SCALE = 64 ** -0.5
TCH = [(0, 512), (512, 512), (1024, 512), (1536, 512), (2048, 256)]
NAUG = 3664
NACT = 2
STOP = None
HOPLIMIT = None
SOLVE_F32 = True


class _Stop(Exception):
    pass


def _chk(tag):
    if STOP == tag:
        raise _Stop()


class Tok:
    __slots__ = ("sem", "val", "clock")

    def __init__(self, sem, val, clock):
        self.sem = sem
        self.val = val
        self.clock = clock


class Res:
    __slots__ = ("name", "w", "r", "excl")

    def __init__(self, name="", excl=False):
        self.name = name
        self.w = None
        self.r = {}
        self.excl = excl


class Prog:
    NDMA = 6

    def __init__(self, nc):
        self.nc = nc
        self.engs = {"pe": nc.tensor, "act": nc.scalar, "dve": nc.vector,
                     "pool": nc.gpsimd, "sp": nc.sync}
        self.sem = {}
        self.cnt = {}
        self.clock = {}
        for k in self.engs:
            self.sem[k] = nc.alloc_semaphore("s_" + k)
            self.cnt[k] = 0
            self.clock[k] = {}
        self.dq = {}
        for q in ("sp", "pool", "act"):
            self.dq[q] = {"i": 0, "slots": []}
            for j in range(self.NDMA):
                nm = "d_%s%d" % (q, j)
                self.sem[nm] = nc.alloc_semaphore(nm)
                self.dq[q]["slots"].append({"sem": nm, "cnt": 0, "tok": None})
        self.nwaits = 0
        self.nops = 0

    def _need(self, e, toks, raw_ids):
        clk = self.clock[e]
        best = {}
        for t in toks:
            if t is None:
                continue
            if t.sem == e and e == "pe" and id(t) not in raw_ids:
                continue
            if clk.get(t.sem, 0) >= t.val:
                continue
            b = best.get(t.sem)
            if b is None or b.val < t.val:
                best[t.sem] = t
        for s, t in best.items():
            if clk.get(s, 0) >= t.val:
                continue
            self.engs[e].wait_ge(self.sem[s], t.val)
            self.nwaits += 1
            for ks, kv in t.clock.items():
                if clk.get(ks, 0) < kv:
                    clk[ks] = kv
            if clk.get(s, 0) < t.val:
                clk[s] = t.val

    def _deps(self, reads, writes):
        toks = []
        raw = set()
        for r in reads:
            if r.w is not None:
                toks.append(r.w)
                raw.add(id(r.w))
            if r.excl:
                toks.extend(r.r.values())
        for w in writes:
            if w.w is not None:
                toks.append(w.w)
            toks.extend(w.r.values())
        return toks, raw

    def _commit(self, tok, reads, writes):
        for r in reads:
            o = r.r.get(tok.sem)
            if o is None or o.val < tok.val:
                r.r[tok.sem] = tok
        for w in writes:
            w.w = tok
            w.r = {}

    def op(self, e, fn, reads=(), writes=()):
        toks, raw = self._deps(reads, writes)
        self._need(e, toks, raw)
        ins = fn(self.engs[e])
        self.cnt[e] += 1
        ins.then_inc(self.sem[e], 1)
        self.nops += 1
        tok = Tok(e, self.cnt[e], dict(self.clock[e]))
        self._commit(tok, reads, writes)
        return tok

    def ops(self, e, fns, reads=(), writes=()):
        toks, raw = self._deps(reads, writes)
        self._need(e, toks, raw)
        ins = None
        for fn in fns:
            ins = fn(self.engs[e])
        self.cnt[e] += 1
        ins.then_inc(self.sem[e], 1)
        self.nops += len(fns)
        tok = Tok(e, self.cnt[e], dict(self.clock[e]))
        self._commit(tok, reads, writes)
        return tok

    def dma(self, q, out, in_, reads=(), writes=(), **kw):
        d = self.dq[q]
        slot = d["slots"][d["i"] % self.NDMA]
        d["i"] += 1
        toks, raw = self._deps(reads, writes)
        if slot["tok"] is not None:
            toks.append(slot["tok"])
        self._need(q, toks, set(id(t) for t in toks))
        ins = self.engs[q].dma_start(out=out, in_=in_, **kw)
        slot["cnt"] += 16
        ins.then_inc(self.sem[slot["sem"]], 16)
        tok = Tok(slot["sem"], slot["cnt"], dict(self.clock[q]))
        slot["tok"] = tok
        self._commit(tok, reads, writes)
        return tok

    def barrier(self):
        toks = []
        for k in self.engs:
            if self.cnt[k] > 0:
                toks.append(Tok(k, self.cnt[k], dict(self.clock[k])))
        for q in self.dq.values():
            for s in q["slots"]:
                if s["tok"] is not None:
                    toks.append(s["tok"])
        for e in self.engs:
            self._need(e, toks, set(id(t) for t in toks if t.sem != e))


class Rot:
    def __init__(self, items):
        self.items = items
        self.i = 0

    def next(self):
        it = self.items[self.i % len(self.items)]
        self.i += 1
        return it


def _partner(dims):
    h = dims // 2
    half = h // 2
    p = np.zeros(dims, np.int64)
    sgn = np.zeros(dims, np.float32)
    for base in (0, h):
        for i in range(h):
            if i < half:
                p[base + i] = base + i + half
                sgn[base + i] = -1.0
            else:
                p[base + i] = base + i - half
                sgn[base + i] = 1.0
    return p, sgn


def _rope_tables(dims):
    h = dims // 2
    half = h // 2
    t = np.arange(SEQ)
    rows = (t // 64).astype(np.float32)
    cols = (t % 64).astype(np.float32)
    inv = (np.float32(10000.0) ** (-np.arange(0, h, 2, dtype=np.float32) / np.float32(h))).astype(np.float32)
    cos = np.zeros((dims, SEQ), np.float32)
    sin = np.zeros((dims, SEQ), np.float32)
    _, sgn = _partner(dims)
    for base, pos in ((0, rows), (h, cols)):
        ang = (pos[None, :] * inv[:, None]).astype(np.float32)
        for i in range(h):
            cos[base + i] = np.cos(ang[i % half])
            sin[base + i] = np.sin(ang[i % half]) * sgn[base + i]
    return cos, sin


def _gdn_masks():
    i = np.arange(64)
    m = {}
    trit = np.zeros((64, 2, 64), np.float32)
    trit[:, 0, :] = (i[:, None] <= i[None, :])
    trit[:, 1, :] = (i[:, None] >= i[None, :])
    omt = np.zeros((64, 2, 64), np.float32)
    omt[:, 0, :] = (i[:, None] > i[None, :])
    omt[:, 1, :] = (i[:, None] < i[None, :])
    nstrict = np.zeros((64, 2, 64), np.float32)
    nstrict[:, 0, :] = -(i[:, None] > i[None, :]).astype(np.float32)
    nstrict[:, 1, :] = -(i[:, None] < i[None, :]).astype(np.float32)
    inclt = np.zeros((64, 2, 64), np.float32)
    inclt[:, 0, :] = (i[None, :] >= i[:, None])
    inclt[:, 1, :] = (i[None, :] <= i[:, None])
    rep = lambda a: np.ascontiguousarray(np.repeat(a, 4, axis=1))
    m["trit2"] = trit
    m["omt2"] = omt
    m["trit8"] = rep(trit)
    m["omt8"] = rep(omt)
    m["nstrict8"] = rep(nstrict)
    m["inclt8"] = rep(inclt)
    m["eye8"] = np.ascontiguousarray(np.broadcast_to(np.eye(64, dtype=np.float32)[:, None, :], (64, 8, 64)))
    m["mbd2"] = np.ascontiguousarray((1.0 + nstrict) * np.float32(-30000.0))
    m["mbdt2"] = np.ascontiguousarray((1.0 - inclt) * np.float32(-30000.0))
    return m


def _host_prepare(inp):
    f = lambda a: np.ascontiguousarray(np.asarray(a, dtype=np.float32))
    g = {}
    w_in = f(inp["w_in"])
    o = 0
    cq = slice(0, 384)
    ckv = slice(384, 640)
    kr0 = 640
    gq0 = 672
    gk0 = 1056
    gv0 = 1184
    gd0 = 1312
    b0 = 2080
    a0 = 2088
    gate0 = 2096
    pm32, _ = _partner(32)
    pm64, _ = _partner(64)
    cols = []
    cols += list(range(0, 384))
    cols += list(range(384, 640))
    cols += [kr0 + i for i in range(32)]
    cols += [kr0 + int(pm32[i]) for i in range(32)]
    cols += [gq0 + i for i in range(384)]
    cols += [gq0 + hh * 64 + int(pm64[i]) for hh in range(6) for i in range(64)]
    cols += [gk0 + i for i in range(128)]
    cols += [gk0 + hh * 64 + int(pm64[i]) for hh in range(2) for i in range(64)]
    cols += [gd0 + i for i in range(768)]
    cols += [gate0 + i for i in range(1024)]
    cols += [gv0 + i for i in range(128)]
    cols += [b0 + i for i in range(8)]
    cols += [a0 + i for i in range(8)]
    cols = np.asarray(cols)
    assert cols.shape[0] == NAUG
    g["w_in_aug"] = np.ascontiguousarray(w_in[:, :, cols])
    w_uq = f(inp["mla_w_uq"])
    uc = []
    for hh in range(6):
        uc += [hh * 96 + i for i in range(96)]
        uc += [hh * 96 + i for i in range(64)] + [hh * 96 + 64 + int(pm32[i]) for i in range(32)]
    g["w_uq_aug"] = np.ascontiguousarray(w_uq[:, :, np.asarray(uc)])
    w_ukv = f(inp["mla_w_ukv"])
    kc = [hh * 128 + i for hh in range(6) for i in range(64)] + [hh * 128 + 64 + i for hh in range(6) for i in range(64)]
    g["w_ukv_aug"] = np.ascontiguousarray(w_ukv[:, :, np.asarray(kc)])
    g["w_ada"] = f(inp["w_ada"])
    g["w_out"] = f(inp["w_out"])
    b_ada = f(inp["b_ada"])
    g["b_ada_fm"] = np.ascontiguousarray(b_ada[:, :2048].reshape(2, 16, 128).transpose(0, 2, 1))
    g["b_ada_gt"] = np.ascontiguousarray(b_ada[:, 2048:].reshape(2, 1, 1024))
    g["norm_g_fm"] = np.ascontiguousarray(f(inp["norm_g"]).reshape(2, 8, 128).transpose(0, 2, 1))
    g["qn_g_fm"] = np.ascontiguousarray(f(inp["mla_qn_g"]).reshape(2, 3, 128).transpose(0, 2, 1))
    g["kvn_g_fm"] = np.ascontiguousarray(f(inp["mla_kvn_g"]).reshape(2, 2, 128).transpose(0, 2, 1))
    qg = f(inp["gqa_qn_g"])
    kg = f(inp["gqa_kn_g"])
    gq = np.zeros((2, 128, 4), np.float32)
    for l in range(2):
        gq[l, :, 0] = np.tile(qg[l], 2)
        gq[l, :, 1] = np.tile(qg[l][pm64], 2)
        gq[l, :, 2] = np.tile(kg[l], 2)
        gq[l, :, 3] = np.tile(kg[l][pm64], 2)
    g["gq_g"] = gq
    g["conv_fm"] = np.ascontiguousarray(f(inp["gdn_conv_w"]).reshape(2, 5, 6, 128).transpose(0, 3, 2, 1))
    g["alog_bc"] = np.ascontiguousarray(np.broadcast_to(f(inp["gdn_a_log"]).reshape(2, 1, 8), (2, 128, 8)))
    g["dtb_bc"] = np.ascontiguousarray(np.broadcast_to(f(inp["gdn_dt_bias"]).reshape(2, 1, 8), (2, 128, 8)))
    g["ong_bc"] = np.ascontiguousarray(np.broadcast_to(f(inp["gdn_on_g"]).reshape(2, 1, 64), (2, 64, 64)))
    g["final_g_bc"] = np.ascontiguousarray(np.broadcast_to(f(inp["final_g"]).reshape(1, 1024), (128, 1024)))
    cM, sM = _rope_tables(32)
    cosM = np.ones((96, SEQ), np.float32)
    sinM = np.zeros((96, SEQ), np.float32)
    cosM[64:] = cM
    sinM[64:] = sM
    g["cosM"] = cosM
    g["sinM"] = sinM
    cG, sG = _rope_tables(64)
    g["cosG"] = np.ascontiguousarray(np.concatenate([cG, cG], 0))
    g["sinG"] = np.ascontiguousarray(np.concatenate([sG, sG], 0))
    g["ident"] = np.eye(128, dtype=np.float32)
    bd = np.zeros((128, 128), np.float32)
    bd[:64, :64] = 1.0
    bd[64:, 64:] = 1.0
    g["bdones"] = bd
    g.update(_gdn_masks())
    return g


def build_program(debug=False, nb=2, nl=2):
    from contextlib import ExitStack
    nc = bass.Bass("TRN2", target_bir_lowering=False)
    P = Prog(nc)

    def din(name, shape, dt=F32):
        return nc.dram_tensor(name, list(shape), dt, kind="ExternalInput").ap()

    def dscr(name, shape, dt=F32):
        return nc.dram_tensor(name, list(shape), dt, kind=("ExternalOutput" if debug else "Internal")).ap()

    x_in = din("x", [2, SEQ, D])
    ctx_in = din("ctx", [2, CTX, D])
    cT_in = din("cT", [128, 8, 3])
    w_ada = din("w_ada", [2, D, 3 * D])
    b_ada_fm = din("b_ada_fm", [2, 128, 16])
    b_ada_gt = din("b_ada_gt", [2, 1, 1024])
    norm_g_fm = din("norm_g_fm", [2, 128, 8])
    w_in_aug = din("w_in_aug", [2, D, NAUG])
    w_uq_aug = din("w_uq_aug", [2, 384, 1152])
    w_ukv_aug = din("w_ukv_aug", [2, 256, 768])
    qn_g_fm = din("qn_g_fm", [2, 128, 3])
    kvn_g_fm = din("kvn_g_fm", [2, 128, 2])
    gq_g = din("gq_g", [2, 128, 4])
    conv_fm = din("conv_fm", [2, 128, 6, 5])
    alog_bc = din("alog_bc", [2, 128, 8])
    dtb_bc = din("dtb_bc", [2, 128, 8])
    ong_bc = din("ong_bc", [2, 64, 64])
    w_out = din("w_out", [2, D, D])
    final_g_bc = din("final_g_bc", [128, 1024])
    cosM_d = din("cosM", [96, SEQ])
    sinM_d = din("sinM", [96, SEQ])
    cosG_d = din("cosG", [128, SEQ])
    sinG_d = din("sinG", [128, SEQ])
    ident_d = din("ident", [128, 128])
    bdones_d = din("bdones", [128, 128])
    mk_d = {k: din(k, s) for k, s in (("trit2", [64, 2, 64]), ("omt2", [64, 2, 64]), ("trit8", [64, 8, 64]),
                                      ("omt8", [64, 8, 64]), ("nstrict8", [64, 8, 64]), ("inclt8", [64, 8, 64]),
                                      ("eye8", [64, 8, 64]), ("mbd2", [64, 2, 64]), ("mbdt2", [64, 2, 64]))}
    out_d = nc.dram_tensor("out", [2, SEQ, D], F32, kind="ExternalOutput").ap()

    x1_d = dscr("x1_s", [SEQ, D])
    ctx1_d = dscr("ctx1_s", [CTX, D])
    gt_d = dscr("gt_s", [2, 3, 128, 1024])
    mla_qT = dscr("mla_qT", [6, 96, NT], BF16)
    mla_kT = dscr("mla_kT", [6, 96, NT], BF16)
    mla_vp = dscr("mla_vp", [6, 128, NTILE, 128], BF16)
    gqa_qT = dscr("gqa_qT", [3, 128, NT], BF16)
    gqa_kT = dscr("gqa_kT", [128, NT], BF16)
    gqa_vp = dscr("gqa_vp", [2, 128, NTILE, 128], BF16)
    gdn_qT = dscr("gdn_qT", [4, 64, NT], BF16)
    gdn_kT = dscr("gdn_kT", [4, 64, NT], BF16)
    gdn_ktm = dscr("gdn_ktm", [NT, 256], BF16)
    gdn_vtm = dscr("gdn_vtm", [NT, 256], BF16)
    gdn_gb = dscr("gdn_gb", [NT, 2, 8])
    sgT = dscr("sgT", [D, NT], BF16)
    ogT = dscr("ogT", [D, NT], BF16)
    R_x1, R_ctx1, R_gt = Res(), Res(), Res()
    R_mq = [Res() for _ in range(6)]
    R_mk = [Res() for _ in range(6)]
    R_mv = [Res() for _ in range(6)]
    R_gq = [Res() for _ in range(3)]
    R_gk = Res()
    R_gv = [Res() for _ in range(2)]
    R_dq, R_dk, R_dktm, R_dvtm, R_dgb = Res(), Res(), Res(), Res(), Res()
    R_sg = [Res() for _ in range(8)]
    R_og = [Res() for _ in range(8)]

    es = ExitStack()

    uid = [0]

    def sb(name, shape, dt=F32, stack=None):
        uid[0] += 1
        t = (stack or es).enter_context(nc.sbuf_tensor("%s_%d" % (name, uid[0]), list(shape), dt))
        return t

    ps = [nc.alloc_psum_tensor("ps%d" % i, [128, 512], F32) for i in range(8)]
    R_ps = [Res("ps%d" % i, excl=True) for i in range(8)]

    ident_f = sb("ident_f", [128, 128]); R_ident_f = Res()
    ident_b = sb("ident_b", [128, 128], BF16); R_ident_b = Res()
    bdones = sb("bdones", [128, 128]); R_bd = Res()
    ones_f = sb("ones_f", [128, 128]); R_ones = Res()
    eps_t = sb("eps_t", [128, 1]); R_eps = Res()
    P.dma("sp", ident_f[:], ident_d, writes=[R_ident_f])
    P.dma("sp", bdones[:], bdones_d, writes=[R_bd])
    P.op("pool", lambda e: e.memset(ones_f[:], 1.0), writes=[R_ones])
    P.op("pool", lambda e: e.memset(eps_t[:], EPS), writes=[R_eps])
    P.op("dve", lambda e: e.tensor_copy(ident_b[:], ident_f[:]), reads=[R_ident_f], writes=[R_ident_b])

    norm_g = sb("norm_g", [128, 2, 8]); qn_g = sb("qn_g", [128, 2, 3]); kvn_g = sb("kvn_g", [128, 2, 2])
    gqg = sb("gqg", [128, 2, 4]); convw = sb("convw", [128, 2, 6, 5]); nealog = sb("nealog", [128, 2, 8])
    dtb = sb("dtb", [128, 2, 8]); ong = sb("ong", [64, 2, 64]); badafm = sb("badafm", [128, 2, 16])
    R_par = Res()
    for l in range(2):
        P.dma("sp", norm_g[:, l, :], norm_g_fm[l], writes=[R_par])
        P.dma("sp", qn_g[:, l, :], qn_g_fm[l], writes=[R_par])
        P.dma("sp", kvn_g[:, l, :], kvn_g_fm[l], writes=[R_par])
        P.dma("sp", gqg[:, l, :], gq_g[l], writes=[R_par])
        P.dma("sp", convw[:, l, :, :], conv_fm[l], writes=[R_par])
        P.dma("sp", nealog[:, l, :], alog_bc[l], writes=[R_par])
        P.dma("sp", dtb[:, l, :], dtb_bc[l], writes=[R_par])
        P.dma("sp", ong[:, l, :], ong_bc[l], writes=[R_par])
        P.dma("sp", badafm[:, l, :], b_ada_fm[l], writes=[R_par])
    P.op("act", lambda e: e.activation(nealog[:], nealog[:], AF.Exp), reads=[R_par], writes=[R_par])
    P.op("dve", lambda e: e.tensor_scalar(nealog[:], nealog[:], -1.0, 0.0, op0=ALU.mult, op1=ALU.add), reads=[R_par], writes=[R_par])

    gm = sb("gm", [128, 2, 8, 3]); shv = sb("shv", [128, 2, 8, 3]); R_mod = Res()
    with ExitStack() as st:
        scin = sb("scin", [128, 8, 3], F32, st); R_scin = Res()
        scbc = sb("scbc", [128, 8, 3, 128], F32, st); R_scbc = Res()
        wa = [(sb("wa%d" % i, [128, 8, 512], F32, st), Res()) for i in range(2)]
        brow = sb("brow", [1, 1024], F32, st); R_brow = Res()
        gst = [(sb("gst%d" % i, [128, 512], F32, st), Res()) for i in range(2)]
        modt = sb("modt", [128, 16, 3], F32, st); R_modt = Res()
        P.dma("sp", scin[:], cT_in, writes=[R_scin])
        P.op("act", lambda e: e.activation(scin[:], scin[:], AF.Silu), reads=[R_scin], writes=[R_scin])
        P.op("dve", lambda e: e.tensor_copy(scbc[:], scin[:].unsqueeze(3).to_broadcast([128, 8, 3, 128])), reads=[R_scin], writes=[R_scbc])
        war = Rot(wa)
        gstr = Rot(gst)
        for l in range(nl):
            wv = w_ada[l].rearrange("(kc p) n -> p kc n", p=128)
            P.dma("sp", brow[:], b_ada_gt[l], writes=[R_brow])
            for piece in range(6):
                wt, Rw = war.next()
                P.dma("sp", wt[:], wv[:, :, piece * 512:(piece + 1) * 512], writes=[Rw])
                if piece < 4:
                    for jb in range(4):
                        blk = piece * 4 + jb
                        P.ops("pe", [(lambda e, kc=kc, jb=jb: e.matmul(ps[0][:, jb * 4:jb * 4 + 3], lhsT=wt[:, kc, jb * 128:(jb + 1) * 128], rhs=scin[:, kc, :], start=(kc == 0), stop=(kc == 7))) for kc in range(8)], reads=[Rw, R_scin], writes=[R_ps[0]])
                    pv = ps[0][:, 0:16].rearrange("p (a b) -> p a b", b=4)[:, :, 0:3]
                    P.op("dve", lambda e: e.tensor_tensor(modt[:, piece * 4:(piece + 1) * 4, :], pv, badafm[:, l, piece * 4:(piece + 1) * 4].unsqueeze(2).to_broadcast([128, 4, 3]), op=ALU.add),
                         reads=[R_ps[0], R_par], writes=[R_modt])
                else:
                    half = piece - 4
                    for j in range(3):
                        bank = 1 + (j % 2)
                        P.ops("pe", [(lambda e, kc=kc, j=j, bank=bank: e.matmul(ps[bank][:, :], lhsT=scbc[:, kc, j, :], rhs=wt[:, kc, :], start=(kc == 0), stop=False)) for kc in range(8)], reads=[Rw, R_scbc], writes=[R_ps[bank]])
                        P.op("pe", lambda e, bank=bank: e.matmul(ps[bank][:, :], lhsT=ones_f[0:1, :], rhs=brow[0:1, half * 512:(half + 1) * 512], start=False, stop=True),
                             reads=[R_ones, R_brow], writes=[R_ps[bank]])
                        g_t, Rg = gstr.next()
                        P.op("act", lambda e, bank=bank: e.copy(g_t[:], ps[bank][:, :]), reads=[R_ps[bank]], writes=[Rg])
                        P.dma("sp", gt_d[l, j, :, half * 512:(half + 1) * 512], g_t[:], reads=[Rg], writes=[R_gt])
            P.op("dve", lambda e: e.tensor_copy(shv[:, l, :, :], modt[:, 0:8, :]), reads=[R_modt], writes=[R_mod])
            P.op("dve", lambda e: e.tensor_scalar(modt[:, 8:16, :], modt[:, 8:16, :], 1.0, 0.0, op0=ALU.add, op1=ALU.add), reads=[R_modt], writes=[R_modt])
            P.op("dve", lambda e: e.tensor_tensor(gm[:, l, :, :], modt[:, 8:16, :], norm_g[:, l, :].unsqueeze(2).to_broadcast([128, 8, 3]), op=ALU.mult),
                 reads=[R_modt, R_par], writes=[R_mod])
        P.barrier()

    def pipe(stages):
        prev = None
        for A_, B_ in stages:
            A_()
            if prev is not None:
                prev()
            prev = B_
        if prev is not None:
            prev()

    def mm(out, lhsT, rhs, start=True, stop=True):
        return lambda e: e.matmul(out, lhsT=lhsT, rhs=rhs, start=start, stop=stop)

    def rstd_from(psum_ap, out_ap, scale, R_in, R_out, npart=128):
        P.op("act", lambda e: e.activation(out_ap, psum_ap, AF.Sqrt, bias=eps_t[0:npart, 0:1], scale=scale), reads=[R_in, R_eps], writes=[R_out])
        P.op("dve", lambda e: e.reciprocal(out_ap, out_ap), reads=[R_out], writes=[R_out])

    try:
      for jb in range(nb):
        for l in range(nl):
            last = (l == 1)
            upd_ctx = not last
            with ExitStack() as st:
                hT = sb("hT", [128, 8, NT], BF16, st); R_hT = [Res() for _ in range(NTILE)]
                cqn = sb("cqn", [128, 3, NT], BF16, st); R_cqn = [Res() for _ in range(5)]
                ckvn = sb("ckvn", [128, 2, NT], BF16, st); R_ckvn = [Res() for _ in range(5)]
                cosM = sb("cosM", [96, SEQ], F32, st); sinM = sb("sinM", [96, SEQ], F32, st)
                cosG = sb("cosG", [128, SEQ], F32, st); sinG = sb("sinG", [128, SEQ], F32, st)
                R_tab = Res()
                P.dma("sp", cosM[:], cosM_d, writes=[R_tab]); P.dma("sp", sinM[:], sinM_d, writes=[R_tab])
                P.dma("sp", cosG[:], cosG_d, writes=[R_tab]); P.dma("sp", sinG[:], sinG_d, writes=[R_tab])
                with ExitStack() as st1:
                    xts = Rot([(sb("xt%d" % i, [128, D], F32, st1), Res()) for i in range(3)])
                    xns = Rot([(sb("xn%d" % i, [128, D], BF16, st1), Res()) for i in range(2)])
                    junk = sb("junk", [128, D], BF16, st1); R_junk = Res()
                    sst = Rot([(sb("ss%d" % i, [128, 1], F32, st1), Res()) for i in range(3)])
                    for t in range(NTILE):
                        xt, Rx = xts.next()
                        if t < 16:
                            src = x_in[jb, t * 128:(t + 1) * 128, :] if l == 0 else x1_d[t * 128:(t + 1) * 128, :]
                            rsrc = [] if l == 0 else [R_x1]
                            j = jb
                        else:
                            tt = t - 16
                            src = ctx_in[jb, tt * 128:(tt + 1) * 128, :] if l == 0 else ctx1_d[tt * 128:(tt + 1) * 128, :]
                            rsrc = [] if l == 0 else [R_ctx1]
                            j = 2
                        P.dma("sp", xt[:], src, reads=rsrc, writes=[Rx])
                        ss, Rs = sst.next()
                        P.op("act", lambda e: e.activation(junk[:], xt[:], AF.Square, accum_out=ss[:]), reads=[Rx], writes=[R_junk, Rs])
                        rstd_from(ss[:], ss[:], 1.0 / D, Rs, Rs)
                        xn, Rn = xns.next()
                        P.op("dve", lambda e: e.tensor_scalar(xn[:], xt[:], ss[:, 0:1], 0.0, op0=ALU.mult, op1=ALU.add), reads=[Rx, Rs], writes=[Rn])
                        bank = t % 2
                        pb = ps[bank][:].bitcast(BF16)
                        P.ops("pe", [(lambda e, kc=kc: e.transpose(pb[:, kc * 128:(kc + 1) * 128], xn[:, kc * 128:(kc + 1) * 128], ident_b[:])) for kc in range(8)], reads=[Rn, R_ident_b], writes=[R_ps[bank]])
                        for kc in range(8):
                            if kc % 2 == 0:
                                P.op("act", lambda e, kc=kc: e.activation(hT[:, kc, t * 128:(t + 1) * 128], pb[:, kc * 128:(kc + 1) * 128], AF.Identity,
                                                                           bias=shv[:, l, kc, j:j + 1], scale=gm[:, l, kc, j:j + 1]),
                                     reads=[R_ps[bank], R_mod], writes=[R_hT[t]])
                            else:
                                P.op("dve", lambda e, kc=kc: e.tensor_scalar(hT[:, kc, t * 128:(t + 1) * 128], pb[:, kc * 128:(kc + 1) * 128],
                                                                              gm[:, l, kc, j:j + 1], shv[:, l, kc, j:j + 1], op0=ALU.mult, op1=ALU.add),
                                     reads=[R_ps[bank], R_mod], writes=[R_hT[t]])
                    P.barrier()
                _chk('P1')

                def hT_res(c0, n):
                    return [R_hT[t] for t in range(c0 // 128, (c0 + n) // 128)]

                with ExitStack() as st2:
                    wf = Rot([(sb("wf%d" % i, [128, 8, 128], F32, st2), Res()) for i in range(3)])
                    wb = Rot([(sb("wb%d" % i, [128, 8, 128], BF16, st2), Res()) for i in range(6)])
                    wvv = w_in_aug[l].rearrange("(kc p) n -> p kc n", p=128)
                    tmpf = Rot([(sb("tmpf%d" % i, [128, 512], F32, st2), Res()) for i in range(6)])
                    sqb = Rot([(sb("sqb%d" % i, [128, 512], BF16, st2), Res()) for i in range(4)])
                    rawc = [(sb("rawc%d" % i, [128, 512], F32, st2), Res()) for i in range(3)]
                    stg = Rot([(sb("stg%d" % i, [128, NT], BF16, st2), Res()) for i in range(3)])
                    rawfull = Rot([(sb("rawf%d" % i, [128, NT], F32, st2), Res()) for i in range(2)])
                    yfull = Rot([(sb("yf%d" % i, [128, NT], F32, st2), Res()) for i in range(2)])
                    finb = stg
                    tmall = Rot([(sb("tmall%d" % i, [128, NTILE, 128], BF16, st2), Res()) for i in range(1)])
                    vpall = sb("vpall", [128, NTILE, 2, 128], BF16, st2); R_vpall = Res()
                    baall = sb("baall", [128, NTILE, 16], F32, st2); R_baall = Res()
                    gball = sb("gball", [128, NTILE, 2, 8], F32, st2); R_gball = Res()
                    ones16 = sb("ones16", [128, 128], BF16, st2); bd16 = sb("bd16", [128, 128], BF16, st2); R_c16 = Res()
                    P.op("pool", lambda e: e.memset(ones16[:], 1.0), writes=[R_c16])
                    P.op("pool", lambda e: e.tensor_copy(bd16[:], bdones[:]), reads=[R_bd], writes=[R_c16])
                    bankr = Rot(list(range(0, 8)))

                    wlist = ([(i * 128, 128) for i in range(5)] + [(640, 64)]
                             + [x for b_ in range(3) for x in ((704 + b_ * 128, 128), (1088 + b_ * 128, 128))] + [(1472, 128), (1600, 128)]
                             + [(1728 + i * 128, 128) for i in range(6)] + [(2496 + i * 128, 128) for i in range(8)] + [(3520, 128), (3648, 16)])
                    wstate = {"n": 0, "t": {}}

                    def wget(i):
                        while wstate["n"] <= min(i + 2, len(wlist) - 1):
                            j = wstate["n"]
                            c0_, nc_ = wlist[j]
                            wt, Rw = wf.next()
                            P.dma("sp", wt[:, :, 0:nc_], wvv[:, :, c0_:c0_ + nc_], writes=[Rw])
                            wtb, Rwb = wb.next()
                            P.op("pool", lambda e: e.tensor_copy(wtb[:, :, 0:nc_], wt[:, :, 0:nc_]), reads=[Rw], writes=[Rwb])
                            wstate["t"][j] = (wtb, Rwb)
                            wstate["n"] += 1
                        return wstate["t"][i]

                    def proj(wtb, Rwb, m0, m, c0, n):
                        bank = bankr.next()
                        P.ops("pe", [(mm(ps[bank][0:m, 0:n], wtb[:, kc, m0:m0 + m], hT[:, kc, c0:c0 + n], kc == 0, kc == 7)) for kc in range(8)], reads=[Rwb] + hT_res(c0, n), writes=[R_ps[bank]])
                        return bank

                    def rstd_big(psum_ap, out_ap, scale, R_in, R_out):
                        P.op("act", lambda e: e.activation(out_ap, psum_ap, AF.Ln, bias=eps_t[:, 0:1], scale=scale), reads=[R_in, R_eps], writes=[R_out])
                        P.op("act", lambda e: e.activation(out_ap, out_ap, AF.Exp, scale=-0.5), reads=[R_out], writes=[R_out])

                    stages = []
                    wi = 0
                    for (nblk, dst, Rdst, nfeat) in ((3, cqn, R_cqn, 384), (2, ckvn, R_ckvn, 256)):
                        for ci, (c0, n) in enumerate(TCH):
                            st_ = {}

                            def A_(nblk=nblk, c0=c0, n=n, st_=st_, wi=wi):
                                st_["banks"] = [proj(*wget(wi + i), 0, 128, c0, n) for i in range(nblk)]

                            def B_(nblk=nblk, c0=c0, n=n, st_=st_, dst=dst, Rdst=Rdst, nfeat=nfeat, ci=ci):
                                sbank = bankr.next()
                                for i in range(nblk):
                                    bank = st_["banks"][i]
                                    sq, Rsq = sqb.next()
                                    P.op("act", lambda e: e.activation(sq[:, 0:n], ps[bank][:, 0:n], AF.Square), reads=[R_ps[bank]], writes=[Rsq])
                                    P.op("dve", lambda e: e.tensor_copy(rawc[i][0][:, 0:n], ps[bank][:, 0:n]), reads=[R_ps[bank]], writes=[rawc[i][1]])
                                    P.op("pe", mm(ps[sbank][:, 0:n], ones16[:, :], sq[:, 0:n], i == 0, i == nblk - 1), reads=[R_c16, Rsq], writes=[R_ps[sbank]])
                                rs, Rrs = tmpf.next()
                                rstd_big(ps[sbank][:, 0:n], rs[:, 0:n], 1.0 / nfeat, R_ps[sbank], Rrs)
                                for i in range(nblk):
                                    P.op("dve" if i % 2 == 0 else "pool", lambda e: e.tensor_tensor(dst[:, i, c0:c0 + n], rawc[i][0][:, 0:n], rs[:, 0:n], op=ALU.mult),
                                         reads=[rawc[i][1], Rrs], writes=[Rdst[ci]])
                            stages.append((A_, B_))
                        wi += nblk
                    kst, Rkst = stg.next()
                    for ci, (c0, n) in enumerate(TCH):
                        st_ = {}

                        def A_(c0=c0, n=n, ci=ci, st_=st_):
                            wtb, Rwb = wget(5)
                            st_["a"] = proj(wtb, Rwb, 0, 32, c0, n)
                            if ci < 4:
                                st_["b"] = proj(wtb, Rwb, 32, 32, c0, n)

                        def B_(c0=c0, n=n, ci=ci, st_=st_):
                            ba = st_["a"]
                            if ci < 4:
                                bb = st_["b"]
                                t1, R1 = tmpf.next()
                                t2, R2 = tmpf.next()
                                P.op("dve", lambda e: e.tensor_tensor(t1[0:32, 0:n], ps[ba][0:32, 0:n], cosM[64:96, c0:c0 + n], op=ALU.mult), reads=[R_ps[ba], R_tab], writes=[R1])
                                P.op("dve", lambda e: e.tensor_tensor(t2[0:32, 0:n], ps[bb][0:32, 0:n], sinM[64:96, c0:c0 + n], op=ALU.mult), reads=[R_ps[bb], R_tab], writes=[R2])
                                P.op("pool", lambda e: e.tensor_tensor(kst[0:32, c0:c0 + n], t1[0:32, 0:n], t2[0:32, 0:n], op=ALU.add), reads=[R1, R2], writes=[Rkst])
                            else:
                                P.op("act", lambda e: e.copy(kst[0:32, c0:c0 + n], ps[ba][0:32, 0:n]), reads=[R_ps[ba]], writes=[Rkst])
                                for hh in range(6):
                                    P.dma("sp", mla_kT[hh, 64:96, :], kst[0:32, :], reads=[Rkst], writes=[R_mk[hh]])
                        stages.append((A_, B_))
                    for blk in range(4):
                        is_k = blk == 3
                        wia = 6 + 2 * blk
                        gi = 2 if is_k else 0
                        qst, Rqst = stg.next()
                        for ci, (c0, n) in enumerate(TCH):
                            if ci == 4 and (not is_k) and (not upd_ctx):
                                continue
                            st_ = {}

                            def A_(c0=c0, n=n, ci=ci, st_=st_, wia=wia):
                                st_["a"] = proj(*wget(wia), 0, 128, c0, n)
                                if ci < 4:
                                    st_["b"] = proj(*wget(wia + 1), 0, 128, c0, n)

                            def B_(c0=c0, n=n, ci=ci, st_=st_, gi=gi, qst=qst, Rqst=Rqst, is_k=is_k, blk=blk):
                                ba = st_["a"]
                                sq, Rsq = sqb.next()
                                P.op("act", lambda e: e.activation(sq[:, 0:n], ps[ba][:, 0:n], AF.Square), reads=[R_ps[ba]], writes=[Rsq])
                                sbank = bankr.next()
                                P.op("pe", mm(ps[sbank][:, 0:n], bd16[:, :], sq[:, 0:n]), reads=[R_c16, Rsq], writes=[R_ps[sbank]])
                                rs, Rrs = tmpf.next()
                                rstd_big(ps[sbank][:, 0:n], rs[:, 0:n], 1.0 / 64, R_ps[sbank], Rrs)
                                if ci < 4:
                                    bb = st_["b"]
                                    t1, R1 = tmpf.next()
                                    t2, R2 = tmpf.next()
                                    P.op("dve", lambda e: e.scalar_tensor_tensor(t1[:, 0:n], ps[ba][:, 0:n], gqg[:, l, gi:gi + 1], cosG[:, c0:c0 + n], op0=ALU.mult, op1=ALU.mult),
                                         reads=[R_ps[ba], R_par, R_tab], writes=[R1])
                                    P.op("dve", lambda e: e.scalar_tensor_tensor(t2[:, 0:n], ps[bb][:, 0:n], gqg[:, l, gi + 1:gi + 2], sinG[:, c0:c0 + n], op0=ALU.mult, op1=ALU.mult),
                                         reads=[R_ps[bb], R_par, R_tab], writes=[R2])
                                    P.op("pool", lambda e: e.tensor_tensor(t1[:, 0:n], t1[:, 0:n], t2[:, 0:n], op=ALU.add), reads=[R1, R2], writes=[R1])
                                    P.op("pool", lambda e: e.tensor_tensor(qst[:, c0:c0 + n], t1[:, 0:n], rs[:, 0:n], op=ALU.mult), reads=[R1, Rrs], writes=[Rqst])
                                else:
                                    P.op("dve", lambda e: e.scalar_tensor_tensor(qst[:, c0:c0 + n], ps[ba][:, 0:n], gqg[:, l, gi:gi + 1], rs[:, 0:n], op0=ALU.mult, op1=ALU.mult),
                                         reads=[R_ps[ba], R_par, Rrs], writes=[Rqst])
                                lastc = (ci == 4) or (ci == 3 and (not is_k) and (not upd_ctx))
                                if lastc:
                                    if is_k:
                                        P.dma("sp", gqa_kT[:, :], qst[:, :], reads=[Rqst], writes=[R_gk])
                                    else:
                                        ncq = NT if upd_ctx else SEQ
                                        P.dma("sp", gqa_qT[blk, :, 0:ncq], qst[:, 0:ncq], reads=[Rqst], writes=[R_gq[blk]])
                            stages.append((A_, B_))
                    for blk in range(6):
                        st_ = {}

                        def A_(blk=blk, st_=st_):
                            wtb, Rwb = wget(14 + blk)
                            raw, Rraw = rawfull.next()
                            st_["raw"] = (raw, Rraw)
                            for ci, (c0, n) in enumerate(TCH):
                                ba = proj(wtb, Rwb, 0, 128, c0, n)
                                P.op("act", lambda e: e.copy(raw[:, c0:c0 + n], ps[ba][:, 0:n]), reads=[R_ps[ba]], writes=[Rraw])

                        def B_(blk=blk, st_=st_):
                            raw, Rraw = st_["raw"]
                            y, Ry = yfull.next()
                            for (s0, L) in ((0, SEQ), (SEQ, CTX)):
                                P.op("dve", lambda e: e.tensor_scalar(y[:, s0:s0 + L], raw[:, s0:s0 + L], convw[:, l, blk, 2:3], 0.0, op0=ALU.mult, op1=ALU.add),
                                     reads=[Rraw, R_par], writes=[Ry])
                                for k in (0, 1, 3, 4):
                                    s_ = k - 2
                                    o_lo, o_hi = s0 + max(0, -s_), s0 + L - max(0, s_)
                                    i_lo, i_hi = s0 + max(0, s_), s0 + L - max(0, -s_)
                                    P.op("dve", lambda e: e.scalar_tensor_tensor(y[:, o_lo:o_hi], raw[:, i_lo:i_hi], convw[:, l, blk, k:k + 1], y[:, o_lo:o_hi], op0=ALU.mult, op1=ALU.add),
                                         reads=[Rraw, R_par, Ry], writes=[Ry])
                            fin, Rfin = finb.next()
                            if blk < 4:
                                P.op("act", lambda e: e.activation(y[:, :], y[:, :], AF.Silu), reads=[Ry], writes=[Ry])
                                for ci, (c0, n) in enumerate(TCH):
                                    sq, Rsq = sqb.next()
                                    P.op("act", lambda e: e.activation(sq[:, 0:n], y[:, c0:c0 + n], AF.Square), reads=[Ry], writes=[Rsq])
                                    sbank = bankr.next()
                                    P.op("pe", mm(ps[sbank][:, 0:n], bd16[:, :], sq[:, 0:n]), reads=[R_c16, Rsq], writes=[R_ps[sbank]])
                                    rs, Rrs = tmpf.next()
                                    rstd_big(ps[sbank][:, 0:n], rs[:, 0:n], 1.0, R_ps[sbank], Rrs)
                                    sc_ = 0.125 if blk < 2 else 1.0
                                    P.op("pool", lambda e: e.tensor_scalar(rs[:, 0:n], rs[:, 0:n], sc_, 0.0, op0=ALU.mult, op1=ALU.add), reads=[Rrs], writes=[Rrs])
                                    P.op("pool", lambda e: e.tensor_tensor(fin[:, c0:c0 + n], y[:, c0:c0 + n], rs[:, 0:n], op=ALU.mult), reads=[Ry, Rrs], writes=[Rfin])
                            else:
                                P.op("act", lambda e: e.activation(fin[:, :], y[:, :], AF.Silu), reads=[Ry], writes=[Rfin])
                            if blk < 2:
                                for hh in range(2):
                                    P.dma("sp", gdn_qT[blk * 2 + hh, :, :], fin[hh * 64:(hh + 1) * 64, :], reads=[Rfin], writes=[R_dq])
                            elif blk < 4:
                                for hh in range(2):
                                    P.dma("sp", gdn_kT[(blk - 2) * 2 + hh, :, :], fin[hh * 64:(hh + 1) * 64, :], reads=[Rfin], writes=[R_dk])
                            if blk >= 2:
                                dst_tm, Rtm = (gdn_ktm, R_dktm) if blk < 4 else (gdn_vtm, R_dvtm)
                                cofs = (blk % 2) * 128
                                tms, Rtms = tmall.next()
                                for t in range(NTILE):
                                    if t % 4 == 0:
                                        bank = bankr.next()
                                    pbk = ps[bank][:].bitcast(BF16)
                                    P.op("pe", lambda e, t=t: e.transpose(pbk[:, (t % 4) * 128:(t % 4 + 1) * 128], fin[:, t * 128:(t + 1) * 128], ident_b[:]),
                                         reads=[Rfin, R_ident_b], writes=[R_ps[bank]])
                                    if t % 4 == 3 or t == NTILE - 1:
                                        tlo = (t // 4) * 4
                                        nt_ = t - tlo + 1
                                        if (t // 4) % 2:
                                            P.op("act", lambda e: e.copy(tms[:, tlo:tlo + nt_, :], pbk[:, 0:nt_ * 128].rearrange("p (a b) -> p a b", b=128)), reads=[R_ps[bank]], writes=[Rtms])
                                        else:
                                            P.op("dve", lambda e: e.tensor_copy(tms[:, tlo:tlo + nt_, :], pbk[:, 0:nt_ * 128].rearrange("p (a b) -> p a b", b=128)), reads=[R_ps[bank]], writes=[Rtms])
                                dvw = dst_tm[:, cofs:cofs + 128].rearrange("(t p) c -> p t c", p=128)
                                for t3 in range(0, NTILE, 3):
                                    P.dma("sp", dvw[:, t3:t3 + 3, :], tms[:, t3:t3 + 3, :], reads=[Rtms], writes=[Rtm])
                        stages.append((A_, B_))
                    for blk in range(8):
                        gs, Rgs = stg.next() if blk % 3 == 0 else (None, None)
                        st_g = {}
                        for ci, (c0, n) in enumerate(TCH):
                            if ci == 4 and not upd_ctx:
                                continue
                            st_ = {}

                            def A_(blk=blk, c0=c0, n=n, st_=st_):
                                st_["a"] = proj(*wget(20 + blk), 0, 128, c0, n)

                            def B_(blk=blk, c0=c0, n=n, ci=ci, st_=st_, st_g=st_g):
                                if "gs" not in st_g:
                                    st_g["gs"] = stg.next()
                                gs, Rgs = st_g["gs"]
                                ba = st_["a"]
                                P.op("act", lambda e: e.activation(gs[:, c0:c0 + n], ps[ba][:, 0:n], AF.Silu), reads=[R_ps[ba]], writes=[Rgs])
                                if ci == (4 if upd_ctx else 3):
                                    ncg = NT if upd_ctx else SEQ
                                    P.dma("sp", sgT[blk * 128:(blk + 1) * 128, 0:ncg], gs[:, 0:ncg], reads=[Rgs], writes=[R_sg[blk]])
                            stages.append((A_, B_))
                    pipe(stages)
                    wtb, Rwb = wget(28)
                    wtb2, Rwb2 = wget(29)
                    P.op("pool", lambda e: e.memset(vpall[:], 1.0), writes=[R_vpall])
                    for t in range(NTILE):
                        bank = bankr.next()
                        P.ops("pe", [(mm(ps[bank][:, 0:128], hT[:, kc, t * 128:(t + 1) * 128], wtb[:, kc, :], kc == 0, kc == 7)) for kc in range(8)], reads=[Rwb, R_hT[t]], writes=[R_ps[bank]])
                        P.ops("pe", [(mm(ps[bank][:, 128:144], hT[:, kc, t * 128:(t + 1) * 128], wtb2[:, kc, 0:16], kc == 0, kc == 7)) for kc in range(8)], reads=[Rwb2, R_hT[t]], writes=[R_ps[bank]])
                        P.op("dve", lambda e, bank=bank, t=t: e.tensor_copy(vpall[:, t, :, 0:64], ps[bank][:, 0:128].rearrange("p (g d) -> p g d", g=2)), reads=[R_ps[bank]], writes=[R_vpall])
                        P.op("act", lambda e, bank=bank, t=t: e.copy(baall[:, t, :], ps[bank][:, 128:144]), reads=[R_ps[bank]], writes=[R_baall])
                    for gg in range(2):
                        P.dma("sp", gqa_vp[gg], vpall[:, :, gg, :], reads=[R_vpall], writes=[R_gv[gg]])
                    bav = baall[:, :, :].rearrange("p t (x d h) -> p t x d h", x=2, d=2)
                    gv_ = gball[:, :, :, :].rearrange("p t d (x h) -> p t d x h", x=2)
                    P.op("act", lambda e: e.activation(gv_[:, :, :, 1, :], bav[:, :, 0, :, :], AF.Sigmoid), reads=[R_baall], writes=[R_gball])
                    P.op("dve", lambda e: e.tensor_tensor(bav[:, :, 1, :, :], bav[:, :, 1, :, :], dtb[:, l, :].rearrange("p (d h) -> p d h", d=2).unsqueeze(1).to_broadcast([128, NTILE, 2, 4]), op=ALU.add),
                         reads=[R_baall, R_par], writes=[R_baall])
                    P.op("act", lambda e: e.activation(bav[:, :, 1, :, :], bav[:, :, 1, :, :], AF.Exp), reads=[R_baall], writes=[R_baall])
                    P.op("act", lambda e: e.activation(bav[:, :, 1, :, :], bav[:, :, 1, :, :], AF.Ln, bias=1.0), reads=[R_baall], writes=[R_baall])
                    P.op("dve", lambda e: e.tensor_tensor(gv_[:, :, :, 0, :], bav[:, :, 1, :, :], nealog[:, l, :].rearrange("p (d h) -> p d h", d=2).unsqueeze(1).to_broadcast([128, NTILE, 2, 4]), op=ALU.mult),
                         reads=[R_baall, R_par], writes=[R_gball])
                    gbw = gdn_gb.rearrange("(t p) d k -> p t d k", p=128)
                    for t3 in range(0, NTILE, 3):
                        P.dma("sp", gbw[:, t3:t3 + 3, :, :], gball[:, t3:t3 + 3, :, :], reads=[R_gball], writes=[R_dgb])
                    P.barrier()
                _chk('P2')

                with ExitStack() as st3:
                    uqf = sb("uqf", [128, 3, 1152], F32, st3); R_uqf = Res()
                    uqb = sb("uqb", [128, 3, 1152], BF16, st3); R_uqb = Res()
                    ukf = sb("ukf", [128, 2, 768], F32, st3); R_ukf = Res()
                    ukb = sb("ukb", [128, 2, 768], BF16, st3); R_ukb = Res()
                    stg3 = Rot([(sb("stg3_%d" % i, [128, NT], BF16, st3), Res()) for i in range(3)])
                    tmp3 = Rot([(sb("tmp3_%d" % i, [128, 512], F32, st3), Res()) for i in range(4)])
                    mvp = Rot([(sb("mvp%d" % i, [128, 6, 128], BF16, st3), Res()) for i in range(2)])
                    bankr = Rot(list(range(0, 8)))
                    P.dma("sp", uqf[:], w_uq_aug[l].rearrange("(kc p) n -> p kc n", p=128), writes=[R_uqf])
                    P.dma("sp", ukf[:], w_ukv_aug[l].rearrange("(kc p) n -> p kc n", p=128), writes=[R_ukf])
                    for kc in range(3):
                        P.op("dve", lambda e, kc=kc: e.tensor_scalar(uqb[:, kc, :], uqf[:, kc, :], qn_g[:, l, kc:kc + 1], 0.0, op0=ALU.mult, op1=ALU.add), reads=[R_uqf, R_par], writes=[R_uqb])
                    for kc in range(2):
                        P.op("pool", lambda e, kc=kc: e.tensor_scalar(ukb[:, kc, :], ukf[:, kc, :], kvn_g[:, l, kc:kc + 1], 0.0, op0=ALU.mult, op1=ALU.add), reads=[R_ukf, R_par], writes=[R_ukb])
                    st3g = []
                    for hh in range(6):
                        qst, Rqst = stg3.next() if False else (None, None)
                        hold = {}
                        for ci, (c0, n) in enumerate(TCH):
                            if ci == 4 and not upd_ctx:
                                continue
                            st_ = {}

                            def A_(hh=hh, ci=ci, c0=c0, n=n, st_=st_):
                                ba = bankr.next()
                                P.ops("pe", [(mm(ps[ba][0:96, 0:n], uqb[:, kc, hh * 192:hh * 192 + 96], cqn[:, kc, c0:c0 + n], kc == 0, kc == 2)) for kc in range(3)], reads=[R_uqb, R_cqn[ci]], writes=[R_ps[ba]])
                                st_["a"] = ba
                                if ci < 4:
                                    bb = bankr.next()
                                    P.ops("pe", [(mm(ps[bb][0:96, 0:n], uqb[:, kc, hh * 192 + 96:hh * 192 + 192], cqn[:, kc, c0:c0 + n], kc == 0, kc == 2)) for kc in range(3)], reads=[R_uqb, R_cqn[ci]], writes=[R_ps[bb]])
                                    st_["b"] = bb

                            def B_(hh=hh, ci=ci, c0=c0, n=n, st_=st_, hold=hold):
                                if "q" not in hold:
                                    hold["q"] = stg3.next()
                                qst, Rqst = hold["q"]
                                ba = st_["a"]
                                if ci < 4:
                                    bb = st_["b"]
                                    t1, R1 = tmp3.next()
                                    t2, R2 = tmp3.next()
                                    P.op("dve", lambda e: e.tensor_tensor(t1[0:96, 0:n], ps[ba][0:96, 0:n], cosM[:, c0:c0 + n], op=ALU.mult), reads=[R_ps[ba], R_tab], writes=[R1])
                                    P.op("dve", lambda e: e.tensor_tensor(t2[0:96, 0:n], ps[bb][0:96, 0:n], sinM[:, c0:c0 + n], op=ALU.mult), reads=[R_ps[bb], R_tab], writes=[R2])
                                    P.op("pool", lambda e: e.tensor_tensor(qst[0:96, c0:c0 + n], t1[0:96, 0:n], t2[0:96, 0:n], op=ALU.add), reads=[R1, R2], writes=[Rqst])
                                else:
                                    P.op("act", lambda e: e.copy(qst[0:96, c0:c0 + n], ps[ba][0:96, 0:n]), reads=[R_ps[ba]], writes=[Rqst])
                                if ci == (4 if upd_ctx else 3):
                                    ncq = NT if upd_ctx else SEQ
                                    P.dma("pool", mla_qT[hh, :, 0:ncq], qst[0:96, 0:ncq], reads=[Rqst], writes=[R_mq[hh]])
                            st3g.append((A_, B_))
                    for pr in range(3):
                        hold = {}
                        for ci, (c0, n) in enumerate(TCH):
                            st_ = {}

                            def A_(pr=pr, ci=ci, c0=c0, n=n, st_=st_):
                                ba = bankr.next()
                                P.ops("pe", [(mm(ps[ba][:, 0:n], ukb[:, kc, pr * 128:(pr + 1) * 128], ckvn[:, kc, c0:c0 + n], kc == 0, kc == 1)) for kc in range(2)], reads=[R_ukb, R_ckvn[ci]], writes=[R_ps[ba]])
                                st_["a"] = ba

                            def B_(pr=pr, ci=ci, c0=c0, n=n, st_=st_, hold=hold):
                                if "k" not in hold:
                                    hold["k"] = stg3.next()
                                kst, Rkst = hold["k"]
                                ba = st_["a"]
                                if ci % 2:
                                    P.op("act", lambda e: e.copy(kst[:, c0:c0 + n], ps[ba][:, 0:n]), reads=[R_ps[ba]], writes=[Rkst])
                                else:
                                    P.op("dve", lambda e: e.tensor_copy(kst[:, c0:c0 + n], ps[ba][:, 0:n]), reads=[R_ps[ba]], writes=[Rkst])
                                if ci == 4:
                                    for s_ in range(2):
                                        P.dma("pool", mla_kT[pr * 2 + s_, 0:64, :], kst[s_ * 64:(s_ + 1) * 64, :], reads=[Rkst], writes=[R_mk[pr * 2 + s_]])
                            st3g.append((A_, B_))
                    for (mt, Rm) in mvp.items:
                        P.op("pool", lambda e, mt=mt: e.memset(mt[:], 1.0), writes=[Rm])
                    for t in range(NTILE):
                        st_ = {}

                        def A_(t=t, st_=st_):
                            ba = bankr.next()
                            ci = min(t // 4, 4)
                            P.ops("pe", [(mm(ps[ba][:, 0:384], ckvn[:, kc, t * 128:(t + 1) * 128], ukb[:, kc, 384:768], kc == 0, kc == 1)) for kc in range(2)], reads=[R_ukb, R_ckvn[ci]], writes=[R_ps[ba]])
                            st_["a"] = ba

                        def B_(t=t, st_=st_):
                            ba = st_["a"]
                            mt, Rm = mvp.next()
                            P.op("dve", lambda e: e.tensor_copy(mt[:, :, 0:64], ps[ba][:, 0:384].rearrange("p (h d) -> p h d", h=6)), reads=[R_ps[ba]], writes=[Rm])
                            P.dma("pool", mla_vp[:, :, t, :].rearrange("h p c -> p h c"), mt[:, :, :], reads=[Rm], writes=R_mv)
                        st3g.append((A_, B_))
                    pipe(st3g)
                    P.barrier()
                _chk('P3')

            with ExitStack() as st4:
                qb = Rot([(sb("qb%d" % i, [96, NT], BF16, st4), Res()) for i in range(2)])
                kb = Rot([(sb("kb%d" % i, [96, NT], BF16, st4), Res()) for i in range(2)])
                vb = Rot([(sb("vb%d" % i, [128, NTILE, 128], BF16, st4), Res()) for i in range(2)])
                pT = Rot([(sb("pT%d" % i, [128, 512], BF16, st4), Res()) for i in range(4)])
                rec = Rot([(sb("rec%d" % i, [128, 512], F32, st4), Res()) for i in range(2)])
                onr = Rot([(sb("on%d" % i, [64, 512], F32, st4), Res()) for i in range(2)])
                sgb = Rot([(sb("sgb%d" % i, [64, 512], BF16, st4), Res()) for i in range(3)])
                ogb = Rot([(sb("ogb%d" % i, [64, 512], BF16, st4), Res()) for i in range(3)])
                sbank = Rot([0, 1, 2, 3])
                obank = Rot([4, 5])
                jobs = []
                for hh in range(6):
                    jobs.append(dict(q=mla_qT[hh], Rq=R_mq[hh], k=mla_kT[hh], Rk=R_mk[hh], v=mla_vp[hh], Rv=R_mv[hh], d=96, scale=MLA_SCALE, f0=hh * 64, newkv=True))
                for hh in range(6):
                    gg = hh // 3
                    jobs.append(dict(q=gqa_qT[hh // 2, (hh % 2) * 64:(hh % 2) * 64 + 64, :], Rq=R_gq[hh // 2], k=gqa_kT[gg * 64:(gg + 1) * 64, :], Rk=R_gk,
                                     v=gqa_vp[gg], Rv=R_gv[gg], d=64, scale=GQA_SCALE, f0=384 + hh * 64, newkv=(hh % 3 == 0)))
                kvstate = [None]

                def load_job(job):
                    d = job["d"]
                    qt, Rqt = qb.next()
                    ncq = NT if upd_ctx else SEQ
                    P.dma("sp", qt[0:d, 0:ncq], job["q"][:, 0:ncq], reads=[job["Rq"]], writes=[Rqt])
                    if job["newkv"]:
                        kt_, Rkt = kb.next()
                        vt_, Rvt = vb.next()
                        P.dma("sp", kt_[0:d, :], job["k"], reads=[job["Rk"]], writes=[Rkt])
                        P.dma("sp", vt_[:, :, :], job["v"], reads=[job["Rv"]], writes=[Rvt])
                        kvstate[0] = (kt_, Rkt, vt_, Rvt)
                    return (qt, Rqt) + kvstate[0]

                loaded = {0: load_job(jobs[0])}
                for ji, job in enumerate(jobs):
                    d = job["d"]
                    qt, Rqt, kt_, Rkt, vt_, Rvt = loaded.pop(ji)
                    if ji + 1 < len(jobs):
                        loaded[ji + 1] = load_job(jobs[ji + 1])
                    qchunks = [(c0, n, list(range(NTILE))) for (c0, n) in TCH[:4]]
                    if upd_ctx:
                        qchunks.append((SEQ, CTX, [16, 17]))
                    steps = []
                    for qi, (c0, n, kts) in enumerate(qchunks):
                        for ki, kt in enumerate(kts):
                            steps.append((qi, c0, n, kt, ki == 0, ki == len(kts) - 1))
                    obanks = {}
                    sb_of = {}

                    def emit_S(si):
                        qi, c0, n, kt, first, lastk = steps[si]
                        b_ = sbank.next()
                        sb_of[si] = b_
                        P.op("pe", mm(ps[b_][:, 0:n], kt_[0:d, kt * 128:(kt + 1) * 128], qt[0:d, c0:c0 + n]), reads=[Rkt, Rqt], writes=[R_ps[b_]])

                    emit_S(0)
                    if len(steps) > 1:
                        emit_S(1)
                    for si in range(len(steps)):
                        qi, c0, n, kt, first, lastk = steps[si]
                        if si + 2 < len(steps):
                            emit_S(si + 2)
                        b_ = sb_of[si]
                        p_, Rp = pT.next()
                        P.op("act", lambda e, b_=b_, p_=p_, n=n: e.activation(p_[:, 0:n], ps[b_][:, 0:n], AF.Exp, scale=job["scale"]), reads=[R_ps[b_]], writes=[Rp])
                        if first:
                            obanks[qi] = obank.next()
                        ob = obanks[qi]
                        P.op("pe", mm(ps[ob][:, 0:n], vt_[:, kt, :], p_[:, 0:n], first, lastk), reads=[Rvt, Rp], writes=[R_ps[ob]])
                        if lastk:
                            r_, Rr = rec.next()
                            P.op("dve", lambda e, ob=ob, r_=r_, n=n: e.reciprocal(r_[64:128, 0:n], ps[ob][64:128, 0:n]), reads=[R_ps[ob]], writes=[Rr])
                            o_, Ro = onr.next()
                            P.op("dve", lambda e, ob=ob, r_=r_, o_=o_, n=n: e.tensor_tensor(o_[:, 0:n], ps[ob][0:64, 0:n], r_[64:128, 0:n], op=ALU.mult), reads=[R_ps[ob], Rr], writes=[Ro])
                            f0 = job["f0"]
                            s_, Rs_ = sgb.next()
                            P.dma("sp", s_[:, 0:n], sgT[f0:f0 + 64, c0:c0 + n], reads=[R_sg[f0 // 128]], writes=[Rs_])
                            g_, Rg_ = ogb.next()
                            P.op("pool", lambda e, o_=o_, s_=s_, g_=g_, n=n: e.tensor_tensor(g_[:, 0:n], o_[:, 0:n], s_[:, 0:n], op=ALU.mult), reads=[Ro, Rs_], writes=[Rg_])
                            P.dma("pool", ogT[f0:f0 + 64, c0:c0 + n], g_[:, 0:n], reads=[Rg_], writes=[R_og[f0 // 128]])
                P.barrier()
            _chk('P4')

            with ExitStack() as st5:
                mkf = {k: sb("mk_" + k, sh_, F32, st5) for k, sh_ in (("trit2", [64, 2, 64]), ("omt2", [64, 2, 64]), ("trit8", [64, 8, 64]),
                                                                      ("omt8", [64, 8, 64]), ("eye8", [64, 8, 64]), ("mbd2", [64, 2, 64]), ("mbdt2", [64, 2, 64]))}
                mkb = {k: sb("mkb_" + k, [64, 2, 64], BF16, st5) for k in ("trit2", "omt2", "mbd2", "mbdt2")}
                ones_b = sb("ones_b", [64, 64], BF16, st5)
                R_mk_ = Res()
                for k in mkf:
                    P.dma("sp", mkf[k][:], mk_d[k], writes=[R_mk_])
                for k in mkb:
                    P.op("dve", lambda e, k=k: e.tensor_copy(mkb[k][:], mkf[k][:]), reads=[R_mk_], writes=[R_mk_])
                P.op("pool", lambda e: e.memset(ones_b[:], 1.0), writes=[R_mk_])
                S = sb("S", [64, 8, 64], F32, st5); R_S = Res()
                Sb = sb("Sb", [64, 8, 64], BF16, st5); R_Sb = Res()
                oacc = sb("oacc", [64, 36, 4, 64], F32, st5); R_oacc = Res()
                P.op("pool", lambda e: e.memset(S[:], 0.0), writes=[R_S])
                P.op("pool", lambda e: e.memset(Sb[:], 0.0), writes=[R_Sb])
                P.op("pool", lambda e: e.memset(oacc[:], 0.0), writes=[R_oacc])

                def T8(name, n=2, shape=(64, 8, 64), dt=F32):
                    return Rot([(sb("%s%d" % (name, i), list(shape), dt, st5), Res()) for i in range(n)])
                qT8r, aTr, kdr, wTr = T8("qT8", 4, dt=BF16), T8("aT", 4, dt=BF16), T8("kd", 4, dt=BF16), T8("wT", 4, dt=BF16)
                ur = T8("u", 4); egcr = T8("egc", 4, (64, 16))
                inst = []
                for k in range(2):
                    d_ = dict(kT8=T8("kT8_%d" % k, 1, dt=BF16), ktm8=T8("ktm8_%d" % k, 1, dt=BF16), vtm8=T8("vtm8_%d" % k, 1, dt=BF16),
                              gb8=T8("gb8_%d" % k, 1, (64, 2, 8)), gbb=T8("gbb_%d" % k, 1, (64, 2, 4), BF16), nb=T8("nb_%d" % k, 1, (64, 2, 4)),
                              Xg=T8("Xg_%d" % k, 1, dt=BF16), Yg=T8("Yg_%d" % k, 1, dt=BF16), dec=T8("dec_%d" % k, 1), decT=T8("decT_%d" % k, 1),
                              ekd=T8("ekd_%d" % k, 1, (64, 8)), bege=T8("bege_%d" % k, 1, (64, 8)),
                              N=T8("N_%d" % k, 2, dt=(F32 if SOLVE_F32 else BF16)), M=T8("M_%d" % k, 2, dt=(F32 if SOLVE_F32 else BF16)), R=T8("R_%d" % k, 2, dt=(F32 if SOLVE_F32 else BF16)),
                              TTb=T8("TTb_%d" % k, 1, dt=BF16),
                              rv=T8("rv_%d" % k, 1, dt=BF16), rw=T8("rw_%d" % k, 1, dt=BF16), banks=(3 * k, 3 * k + 1, 3 * k + 2))
                    inst.append(d_)
                vnr, t8r = T8("vn", 1, dt=BF16), T8("t8", 2)
                order_f = [32, 33, 34, 35] + list(range(32))
                order_b = [35, 34, 33, 32] + list(range(31, -1, -1))
                v24 = lambda ap: ap.rearrange("p (d h) j -> p d h j", d=2)
                fl = lambda ap: ap.rearrange("p a b -> p (a b)")
                A_out = {}

                def passA(s):
                    I_ = inst[s % 2]
                    X, Y, Z = I_["banks"]
                    cf, cb = order_f[s], order_b[s]
                    qT8, Rq8 = qT8r.next(); kT8, Rk8 = I_["kT8"].next(); ktm8, Rkt8 = I_["ktm8"].next(); vtm8, Rvt8 = I_["vtm8"].next(); gb8, Rgb8 = I_["gb8"].next()
                    for di, c in ((0, cf), (1, cb)):
                        P.dma("sp", qT8[:, di * 4:(di + 1) * 4, :], gdn_qT[:, :, c * 64:(c + 1) * 64].rearrange("h d t -> d h t"), reads=[R_dq], writes=[Rq8])
                        P.dma("sp", kT8[:, di * 4:(di + 1) * 4, :], gdn_kT[:, :, c * 64:(c + 1) * 64].rearrange("h d t -> d h t"), reads=[R_dk], writes=[Rk8])
                        P.dma("sp", ktm8[:, di * 4:(di + 1) * 4, :], gdn_ktm[c * 64:(c + 1) * 64, :].rearrange("p (h d) -> p h d", h=4), reads=[R_dktm], writes=[Rkt8])
                        P.dma("sp", vtm8[:, di * 4:(di + 1) * 4, :], gdn_vtm[c * 64:(c + 1) * 64, :].rearrange("p (h d) -> p h d", h=4), reads=[R_dvtm], writes=[Rvt8])
                        P.dma("sp", gb8[:, di, :], gdn_gb[c * 64:(c + 1) * 64, di, :], reads=[R_dgb], writes=[Rgb8])
                    yield
                    gbb, Rgbb = I_["gbb"].next(); nb, Rnb = I_["nb"].next()
                    P.op("pool", lambda e: e.tensor_copy(gbb[:], gb8[:, :, 0:4]), reads=[Rgb8], writes=[Rgbb])
                    P.op("pool", lambda e: e.tensor_scalar(nb[:], gb8[:, :, 4:8], -1.0, 0.0, op0=ALU.mult, op1=ALU.add), reads=[Rgb8], writes=[Rnb])
                    g_bc = gb8[:, :, 0:4].unsqueeze(3).to_broadcast([64, 2, 4, 64])
                    be_bc = gb8[:, :, 4:8].unsqueeze(3).to_broadcast([64, 2, 4, 64])
                    nb_bc = nb[:, :, :].unsqueeze(3).to_broadcast([64, 2, 4, 64])
                    Xg, RXg = I_["Xg"].next(); Yg, RYg = I_["Yg"].next()
                    P.op("dve", lambda e: e.tensor_tensor(v24(Xg[:]), v24(mkf["trit8"][:]), g_bc, op=ALU.mult), reads=[R_mk_, Rgb8], writes=[RXg])
                    P.op("pool", lambda e: e.tensor_tensor(v24(Yg[:]), v24(mkf["omt8"][:]), g_bc, op=ALU.mult), reads=[R_mk_, Rgb8], writes=[RYg])
                    yield
                    fnsX, fnsY, fnsZ = [], [], []
                    for dh in range(8):
                        di = dh // 4
                        fnsX.append(mm(ps[X][0:64, dh * 64:(dh + 1) * 64], Xg[:, dh, :], mkb["omt2"][:, di, :], True, False))
                        fnsX.append(mm(ps[X][0:64, dh * 64:(dh + 1) * 64], ident_b[0:64, 0:64], mkb["mbd2"][:, di, :], False, True))
                        fnsY.append(mm(ps[Y][0:64, dh * 64:(dh + 1) * 64], Yg[:, dh, :], mkb["trit2"][:, di, :], True, False))
                        fnsY.append(mm(ps[Y][0:64, dh * 64:(dh + 1) * 64], ident_b[0:64, 0:64], mkb["mbdt2"][:, di, :], False, True))
                    for di in range(2):
                        fnsZ.append(mm(ps[Z][0:64, di * 4:(di + 1) * 4], mkb["trit2"][:, di, :], gbb[:, di, :]))
                        fnsZ.append(mm(ps[Z][0:64, 8 + di * 4:8 + (di + 1) * 4], ones_b[:, :], gbb[:, di, :]))
                    P.ops("pe", fnsX, reads=[RXg, R_mk_, R_ident_b], writes=[R_ps[X]])
                    P.ops("pe", fnsY, reads=[RYg, R_mk_, R_ident_b], writes=[R_ps[Y]])
                    P.ops("pe", fnsZ, reads=[R_mk_, Rgbb], writes=[R_ps[Z]])
                    yield
                    dec, Rdec = I_["dec"].next(); decT, RdecT = I_["decT"].next(); egc, Regc = egcr.next(); ekd, Rekd = I_["ekd"].next()
                    P.op("act", lambda e: e.activation(fl(dec[:]), ps[X][0:64, :], AF.Exp), reads=[R_ps[X]], writes=[Rdec])
                    P.op("act", lambda e: e.activation(fl(decT[:]), ps[Y][0:64, :], AF.Exp), reads=[R_ps[Y]], writes=[RdecT])
                    P.op("act", lambda e: e.activation(egc[:], ps[Z][0:64, 0:16], AF.Exp), reads=[R_ps[Z]], writes=[Regc])
                    P.op("dve", lambda e: e.tensor_copy(ekd[:], ps[Z][0:64, 0:8]), reads=[R_ps[Z]], writes=[Rekd])
                    P.op("dve", lambda e: e.tensor_tensor(ekd[:], ps[Z][0:64, 8:16], ekd[:], op=ALU.subtract), reads=[R_ps[Z], Rekd], writes=[Rekd])
                    P.op("act", lambda e: e.activation(ekd[:], ekd[:], AF.Exp), reads=[Rekd], writes=[Rekd])
                    yield
                    P.ops("pe", [(mm(ps[X][0:64, dh * 64:(dh + 1) * 64], kT8[:, dh, :], kT8[:, dh, :])) for dh in range(8)], reads=[Rk8], writes=[R_ps[X]])
                    P.ops("pe", [(mm(ps[Y][0:64, dh * 64:(dh + 1) * 64], kT8[:, dh, :], qT8[:, dh, :])) for dh in range(8)], reads=[Rk8, Rq8], writes=[R_ps[Y]])
                    P.op("pool", lambda e: e.tensor_tensor(v24(dec[:]), v24(dec[:]), nb_bc, op=ALU.mult), reads=[Rdec, Rnb], writes=[Rdec])
                    bege, Rbege = I_["bege"].next()
                    P.op("pool", lambda e: e.tensor_tensor(bege[:].rearrange("p (d h) -> p d h", d=2), gb8[:, :, 4:8], egc[:, 0:8].rearrange("p (d h) -> p d h", d=2), op=ALU.mult), reads=[Rgb8, Regc], writes=[Rbege])
                    yield
                    pX = ps[X][0:64, :].rearrange("p (a b) -> p a b", a=8)
                    pY = ps[Y][0:64, :].rearrange("p (a b) -> p a b", a=8)
                    N0, RN = I_["N"].next()
                    P.op("dve", lambda e: e.tensor_tensor(N0[:], dec[:], pX, op=ALU.mult), reads=[Rdec, R_ps[X]], writes=[RN])
                    aT, RaT = aTr.next()
                    P.op("dve", lambda e: e.tensor_tensor(aT[:], decT[:], pY, op=ALU.mult), reads=[RdecT, R_ps[Y]], writes=[RaT])
                    rv, Rrv = I_["rv"].next(); rw, Rrw = I_["rw"].next(); kd, Rkd = kdr.next()
                    P.op("pool", lambda e: e.tensor_tensor(v24(rv[:]), v24(vtm8[:]), be_bc, op=ALU.mult), reads=[Rvt8, Rgb8], writes=[Rrv])
                    P.op("pool", lambda e: e.tensor_tensor(rw[:], ktm8[:], bege[:].unsqueeze(2).to_broadcast([64, 8, 64]), op=ALU.mult), reads=[Rkt8, Rbege], writes=[Rrw])
                    P.op("pool", lambda e: e.tensor_tensor(kd[:], ktm8[:], ekd[:].unsqueeze(2).to_broadcast([64, 8, 64]), op=ALU.mult), reads=[Rkt8, Rekd], writes=[Rkd])
                    yield
                    pZb = ps[Z][:] if SOLVE_F32 else ps[Z][:].bitcast(BF16)
                    idt_ = ident_f if SOLVE_F32 else ident_b
                    P.ops("pe", [(lambda e, dh=dh: e.transpose(pZb[0:64, dh * 64:(dh + 1) * 64], N0[:, dh, :], idt_[0:64, 0:64])) for dh in range(8)], reads=[RN, R_ident_b, R_ident_f], writes=[R_ps[Z]])
                    yield
                    M0, RM = I_["M"].next()
                    P.op("act", lambda e: e.copy(fl(M0[:]), pZb[0:64, 0:512]), reads=[R_ps[Z]], writes=[RM])
                    Rk, RRk = I_["R"].next()
                    P.op("pool", lambda e: e.tensor_tensor(Rk[:], M0[:], mkf["eye8"][:], op=ALU.add), reads=[RM, R_mk_], writes=[RRk])
                    yield
                    Np, RNp, Mp, RMp = N0, RN, M0, RM
                    for lev in range(1, 6):
                        P.ops("pe", [(mm(ps[X][0:64, dh * 64:(dh + 1) * 64], Mp[:, dh, :], Np[:, dh, :])) for dh in range(8)], reads=[RMp, RNp], writes=[R_ps[X]])
                        if lev < 5:
                            P.ops("pe", [(mm(ps[Y][0:64, dh * 64:(dh + 1) * 64], Np[:, dh, :], Mp[:, dh, :])) for dh in range(8)], reads=[RMp, RNp], writes=[R_ps[Y]])
                        yield
                        Nn, RNn = I_["N"].next()
                        P.op("act", lambda e, Nn=Nn: e.copy(fl(Nn[:]), ps[X][0:64, :]), reads=[R_ps[X]], writes=[RNn])
                        if lev < 5:
                            Mn, RMn = I_["M"].next()
                            P.op("dve", lambda e, Mn=Mn: e.tensor_copy(fl(Mn[:]), ps[Y][0:64, :]), reads=[R_ps[Y]], writes=[RMn])
                        else:
                            Mn, RMn = None, None
                        yield
                        P.ops("pe", [(mm(ps[Z][0:64, dh * 64:(dh + 1) * 64], Nn[:, dh, :], Rk[:, dh, :])) for dh in range(8)], reads=[RNn, RRk], writes=[R_ps[Z]])
                        yield
                        Rn, RRn = I_["R"].next()
                        P.op("dve", lambda e, Rn=Rn, Rk=Rk: e.tensor_tensor(fl(Rn[:]), fl(Rk[:]), ps[Z][0:64, :], op=ALU.add), reads=[RRk, R_ps[Z]], writes=[RRn])
                        yield
                        Rk, RRk = Rn, RRn
                        Np, RNp, Mp, RMp = Nn, RNn, Mn, RMn
                    if SOLVE_F32:
                        TT, RTT = I_["TTb"].next()
                        P.op("act", lambda e: e.copy(fl(TT[:]), fl(Rk[:])), reads=[RRk], writes=[RTT])
                        yield
                    else:
                        TT, RTT = Rk, RRk
                    P.ops("pe", [(mm(ps[X][0:64, dh * 64:(dh + 1) * 64], TT[:, dh, :], rv[:, dh, :])) for dh in range(8)], reads=[RTT, Rrv], writes=[R_ps[X]])
                    P.ops("pe", [(mm(ps[Y][0:64, dh * 64:(dh + 1) * 64], rw[:, dh, :], TT[:, dh, :])) for dh in range(8)], reads=[RTT, Rrw], writes=[R_ps[Y]])
                    yield
                    u, Ru = ur.next(); wT, RwT = wTr.next()
                    P.op("act", lambda e: e.copy(fl(u[:]), ps[X][0:64, :]), reads=[R_ps[X]], writes=[Ru])
                    P.op("dve", lambda e: e.tensor_copy(fl(wT[:]), ps[Y][0:64, :]), reads=[R_ps[Y]], writes=[RwT])
                    A_out[s] = dict(cf=cf, cb=cb, qT8=qT8, Rq8=Rq8, aT=aT, RaT=RaT, kd=kd, Rkd=Rkd, u=u, Ru=Ru, wT=wT, RwT=RwT, egc=egc, Regc=Regc)

                def passB(s):
                    A = A_out.pop(s)
                    P.ops("pe", [(mm(ps[6][0:64, dh * 64:(dh + 1) * 64], A["wT"][:, dh, :], Sb[:, dh, :])) for dh in range(8)], reads=[A["RwT"], R_Sb], writes=[R_ps[6]])
                    yield
                    vn, Rvn = vnr.next()
                    P.op("dve", lambda e: e.tensor_tensor(fl(vn[:]), fl(A["u"][:]), ps[6][0:64, :], op=ALU.subtract), reads=[A["Ru"], R_ps[6]], writes=[Rvn])
                    yield
                    P.ops("pe", [(mm(ps[7][0:64, dh * 64:(dh + 1) * 64], A["qT8"][:, dh, :], Sb[:, dh, :])) for dh in range(8)], reads=[A["Rq8"], R_Sb], writes=[R_ps[7]])
                    P.ops("pe", [(mm(ps[6][0:64, dh * 64:(dh + 1) * 64], A["kd"][:, dh, :], vn[:, dh, :])) for dh in range(8)], reads=[A["Rkd"], Rvn], writes=[R_ps[6]])
                    yield
                    t8, Rt8 = t8r.next()
                    P.op("dve", lambda e: e.tensor_tensor(S[:], S[:], A["egc"][:, 8:16].unsqueeze(2).to_broadcast([64, 8, 64]), op=ALU.mult), reads=[R_S, A["Regc"]], writes=[R_S])
                    P.op("dve", lambda e: e.tensor_tensor(fl(S[:]), fl(S[:]), ps[6][0:64, :], op=ALU.add), reads=[R_S, R_ps[6]], writes=[R_S])
                    P.op("act", lambda e: e.copy(fl(Sb[:]), fl(S[:])), reads=[R_S], writes=[R_Sb])
                    P.op("dve", lambda e: e.tensor_tensor(t8[:], ps[7][0:64, :].rearrange("p (a b) -> p a b", a=8), A["egc"][:, 0:8].unsqueeze(2).to_broadcast([64, 8, 64]), op=ALU.mult), reads=[R_ps[7], A["Regc"]], writes=[Rt8])
                    yield
                    P.ops("pe", [(mm(ps[7][0:64, dh * 64:(dh + 1) * 64], A["aT"][:, dh, :], vn[:, dh, :])) for dh in range(8)], reads=[A["RaT"], Rvn], writes=[R_ps[7]])
                    yield
                    P.op("dve", lambda e: e.tensor_tensor(fl(t8[:]), fl(t8[:]), ps[7][0:64, :], op=ALU.add), reads=[Rt8, R_ps[7]], writes=[Rt8])
                    P.op("pool", lambda e: e.tensor_tensor(oacc[:, A["cf"], :, :], oacc[:, A["cf"], :, :], t8[:, 0:4, :], op=ALU.add), reads=[Rt8, R_oacc], writes=[R_oacc])
                    P.op("pool", lambda e: e.tensor_tensor(oacc[:, A["cb"], :, :], oacc[:, A["cb"], :, :], t8[:, 4:8, :], op=ALU.add), reads=[Rt8, R_oacc], writes=[R_oacc])

                activeA = []
                nextA = 0
                sB = 0
                curB = None
                doneA = set()
                while sB < 36:
                    if (len(activeA) < NACT and nextA < 36 and nextA - sB < 4
                            and all(it[2] >= 14 for it in activeA)):
                        activeA.append([nextA, passA(nextA), 0])
                        nextA += 1
                    if curB is None and sB in doneA:
                        curB = passB(sB)
                    for item in list(activeA):
                        try:
                            next(item[1])
                            item[2] += 1
                            if HOPLIMIT is not None and item[0] == 0 and item[2] >= HOPLIMIT:
                                raise _Stop()
                        except StopIteration:
                            activeA.remove(item)
                            doneA.add(item[0])
                    if curB is not None:
                        try:
                            next(curB)
                        except StopIteration:
                            curB = None
                            sB += 1
                sqo = sb("sqo", [64, 36, 4, 64], F32, st5); R_sqo = Res()
                sso = sb("sso", [64, 144], F32, st5); R_sso = Res()
                P.op("act", lambda e: e.activation(sqo[:].rearrange("p a b c -> p (a b c)"), oacc[:].rearrange("p a b c -> p (a b c)"), AF.Square), reads=[R_oacc], writes=[R_sqo])
                P.op("dve", lambda e: e.tensor_reduce(sso[:], sqo[:].rearrange("p a b c -> p (a b) c"), axis=AX.X, op=ALU.add), reads=[R_sqo], writes=[R_sso])
                rstd_from(sso[:], sso[:], 1.0 / 64, R_sso, R_sso, npart=64)
                P.op("dve", lambda e: e.tensor_tensor(oacc[:].rearrange("p a b c -> p (a b) c"), oacc[:].rearrange("p a b c -> p (a b) c"), sso[:].unsqueeze(2).to_broadcast([64, 144, 64]), op=ALU.mult), reads=[R_oacc, R_sso], writes=[R_oacc])
                P.op("pool", lambda e: e.tensor_tensor(oacc[:].rearrange("p a b c -> p (a b) c"), oacc[:].rearrange("p a b c -> p (a b) c"), ong[:, l, :].unsqueeze(1).to_broadcast([64, 144, 64]), op=ALU.mult), reads=[R_oacc, R_par], writes=[R_oacc])
                sgf = Rot([(sb("sgf%d" % i, [128, NT], BF16, st5), Res()) for i in range(2)])
                ogf = Rot([(sb("ogf%d" % i, [128, NT], BF16, st5), Res()) for i in range(2)])
                bankr = Rot(list(range(8)))
                for pr in range(2):
                    sg_, Rsg_ = sgf.next()
                    ncg = NT if upd_ctx else SEQ
                    P.dma("sp", sg_[:, 0:ncg], sgT[768 + pr * 128:768 + (pr + 1) * 128, 0:ncg], reads=[R_sg[6 + pr]], writes=[Rsg_])
                    og_, Rog_ = ogf.next()
                    for c in range(36):
                        if c >= 32 and not upd_ctx:
                            continue
                        ba = bankr.next()
                        P.op("pe", lambda e, ba=ba, c=c: e.transpose(ps[ba][:, 0:64], oacc[:, c, pr * 2:pr * 2 + 2, :].rearrange("p a b -> p (a b)"), ident_f[0:64, 0:64]), reads=[R_oacc, R_ident_f], writes=[R_ps[ba]])
                        P.op("dve", lambda e, ba=ba, c=c: e.tensor_tensor(og_[:, c * 64:(c + 1) * 64], ps[ba][:, 0:64], sg_[:, c * 64:(c + 1) * 64], op=ALU.mult), reads=[R_ps[ba], Rsg_], writes=[Rog_])
                    ncol = NT if upd_ctx else SEQ
                    P.dma("pool", ogT[768 + pr * 128:768 + (pr + 1) * 128, 0:ncol], og_[:, 0:ncol], reads=[Rog_], writes=[R_og[6 + pr]])
                P.barrier()
            _chk('P5')

            with ExitStack() as st6:
                wof = Rot([(sb("wof%d" % i, [128, 8, 512], F32, st6), Res()) for i in range(2)])
                gtb = Rot([(sb("gtb%d" % i, [128, 1024], F32, st6), Res()) for i in range(2)])
                wog = {}
                variants = [(0, jb)] + ([(1, 2)] if upd_ctx else [])
                wov = w_out[l].rearrange("(kc p) n -> p kc n", p=128)
                for vi, jj in variants:
                    wt_ = sb("wog%d" % vi, [128, 8, 1024], BF16, st6); Rwt_ = Res()
                    g_, Rg_ = gtb.next()
                    P.dma("sp", g_[:], gt_d[l, jj], reads=[R_gt], writes=[Rg_])
                    for half in range(2):
                        wf_, Rwf_ = wof.next()
                        P.dma("sp", wf_[:], wov[:, :, half * 512:(half + 1) * 512], writes=[Rwf_])
                        P.op("dve" if half else "pool", lambda e, wf_=wf_, g_=g_, half=half, wt_=wt_: e.tensor_tensor(wt_[:, :, half * 512:(half + 1) * 512], wf_[:], g_[:, half * 512:(half + 1) * 512].unsqueeze(1).to_broadcast([128, 8, 512]), op=ALU.mult),
                             reads=[Rwf_, Rg_], writes=[Rwt_])
                    wog[vi] = (wt_, Rwt_)
                ogt = Rot([(sb("ogt%d" % i, [128, 8, 128], BF16, st6), Res()) for i in range(3)])
                xt6 = Rot([(sb("xt6_%d" % i, [128, D], F32, st6), Res()) for i in range(3)])
                junk6 = sb("junk6", [128, D], F32, st6); R_junk6 = Res()
                ss6 = Rot([(sb("ss6_%d" % i, [128, 1], F32, st6), Res()) for i in range(2)])
                fgb = sb("fgb", [128, D], F32, st6); R_fgb = Res()
                if last:
                    P.dma("sp", fgb[:], final_g_bc, writes=[R_fgb])
                ogv = ogT.rearrange("(kc p) t -> p kc t", p=128)
                bankp = Rot([(0, 1), (2, 3), (4, 5)])
                ntl = NTILE if upd_ctx else 16
                for t in range(ntl):
                    vi = 0 if t < 16 else 1
                    wt_, Rwt_ = wog[vi]
                    og_, Rog_ = ogt.next()
                    P.dma("sp", og_[:], ogv[:, :, t * 128:(t + 1) * 128], reads=R_og, writes=[Rog_])
                    xt, Rx = xt6.next()
                    if t < 16:
                        src = x_in[jb, t * 128:(t + 1) * 128, :] if l == 0 else x1_d[t * 128:(t + 1) * 128, :]
                        rsrc = [] if l == 0 else [R_x1]
                    else:
                        tt = t - 16
                        src = ctx_in[jb, tt * 128:(tt + 1) * 128, :] if l == 0 else ctx1_d[tt * 128:(tt + 1) * 128, :]
                        rsrc = [] if l == 0 else [R_ctx1]
                    P.dma("sp", xt[:], src, reads=rsrc, writes=[Rx])
                    b0_, b1_ = bankp.next()
                    for half, bk in ((0, b0_), (1, b1_)):
                        P.ops("pe", [(mm(ps[bk][:, :], og_[:, kc, :], wt_[:, kc, half * 512:(half + 1) * 512], kc == 0, kc == 7)) for kc in range(8)], reads=[Rog_, Rwt_], writes=[R_ps[bk]])
                        P.op("dve", lambda e, bk=bk, half=half, xt=xt: e.tensor_tensor(xt[:, half * 512:(half + 1) * 512], xt[:, half * 512:(half + 1) * 512], ps[bk][:, :], op=ALU.add), reads=[Rx, R_ps[bk]], writes=[Rx])
                    if not last:
                        if t < 16:
                            P.dma("pool", x1_d[t * 128:(t + 1) * 128, :], xt[:], reads=[Rx], writes=[R_x1])
                        else:
                            tt = t - 16
                            P.dma("pool", ctx1_d[tt * 128:(tt + 1) * 128, :], xt[:], reads=[Rx], writes=[R_ctx1])
                    else:
                        ss, Rs = ss6.next()
                        P.op("act", lambda e, xt=xt, ss=ss: e.activation(junk6[:], xt[:], AF.Square, accum_out=ss[:]), reads=[Rx], writes=[R_junk6, Rs])
                        rstd_from(ss[:], ss[:], 1.0 / D, Rs, Rs)
                        P.op("dve", lambda e, xt=xt, ss=ss: e.scalar_tensor_tensor(xt[:], xt[:], ss[:, 0:1], fgb[:], op0=ALU.mult, op1=ALU.mult), reads=[Rx, Rs, R_fgb], writes=[Rx])
                        P.dma("pool", out_d[jb, t * 128:(t + 1) * 128, :], xt[:], reads=[Rx])
                P.barrier()
    except _Stop:
        P.barrier()
        return nc, P
    P.barrier()
    es.close()
    return nc, P


_INPUT_NAMES = ["cT", "w_ada", "b_ada_fm", "b_ada_gt", "norm_g_fm", "w_in_aug", "w_uq_aug", "w_ukv_aug", "qn_g_fm", "kvn_g_fm",
                "gq_g", "conv_fm", "alog_bc", "dtb_bc", "ong_bc", "w_out", "final_g_bc", "cosM", "sinM", "cosG", "sinG",
                "ident", "bdones", "trit2", "omt2", "trit8", "omt8", "nstrict8", "inclt8", "eye8", "mbd2", "mbdt2"]


def make_in_maps(inp, cores):
    g = _host_prepare(inp)
    x = np.asarray(inp["x"], dtype=np.float32)
    ctx = np.asarray(inp["ctx"], dtype=np.float32)
    c = np.asarray(inp["c"], dtype=np.float32)
    c_ctx = np.asarray(inp["c_ctx"], dtype=np.float32)
    maps = []
    for core in cores:
        m = {k: g[k] for k in _INPUT_NAMES if k != "cT"}
        b0 = 2 * core
        vecs = np.stack([c[b0], c[b0 + 1], c_ctx], axis=1)
        m["cT"] = np.ascontiguousarray(vecs.reshape(8, 128, 3).transpose(1, 0, 2))
        m["x"] = np.ascontiguousarray(x[b0:b0 + 2])
        m["ctx"] = np.ascontiguousarray(ctx[b0:b0 + 2])
        maps.append(m)
    return maps


def kernel(**inputs):
    nc, _ = build_program()
    cores = list(range(8))
    in_maps = make_in_maps(inputs, cores)
    res = run_bass_kernel_spmd(nc, in_maps, core_ids=cores)
    out = np.concatenate([np.asarray(r["out"], dtype=np.float32) for r in res.results], axis=0)
    return out
```
